# Optimizing a Trainium2 kernel written in Bass

```python
import math
import jax, jax.numpy as jnp
from jax import lax
import numpy as np

D_MODEL = 2048
BATCH = 2
SEQ = 16384
DEPTH = 2

N_HEADS = 32
N_KV_GROUPS = 4
HEADS_PER_GROUP = N_HEADS // N_KV_GROUPS
D_K = 96
D_V = 64
CMP_BLOCK = 32
CMP_STRIDE = 16
CMP_HIDDEN = 256
SLC_BLOCK = 64
N_SLC = 16
WINDOW = 512
Q_BLOCK = 128
N_BUCKETS = 32
MAX_DISTANCE = 128
POOL_WINDOWS = (2, 4, 8, 16)
N_POOL_GROUPS = len(POOL_WINDOWS)
POOL_GROUP = D_MODEL // N_POOL_GROUPS
D_FF = -(-8 * D_MODEL // (3 * 256)) * 256
EPS = 1e-6
NEG_INF = -1e30
FORCE_SCORE = 1e9
N_NSA_LAYERS = (DEPTH + 1) // 2
N_POOL_LAYERS = DEPTH // 2
Q_WIDTH = N_HEADS * D_K
K_WIDTH = N_KV_GROUPS * D_K
V_WIDTH = N_KV_GROUPS * D_V
GATE_WIDTH = 3 * N_HEADS
NSA_SPLITS = (Q_WIDTH, K_WIDTH, V_WIDTH, K_WIDTH, V_WIDTH, K_WIDTH, V_WIDTH, GATE_WIDTH)
NSA_PROJ = sum(NSA_SPLITS)

kernel_name = 'nsa_pool_interleaved_hybrid'


def rms_norm(x, g):
    x32 = x.astype(jnp.float32)
    y = x32 * lax.rsqrt(jnp.mean(x32 * x32, axis=-1, keepdims=True) + EPS)
    return (y * g.astype(jnp.float32)).astype(x.dtype)


def rel_bucket(dist):
    n = jnp.maximum(dist, 0)
    max_exact = N_BUCKETS // 2
    nf = jnp.maximum(n, 1).astype(jnp.float32)
    large = max_exact + (jnp.log(nf / max_exact) / math.log(MAX_DISTANCE / max_exact)
                         * (N_BUCKETS - max_exact)).astype(jnp.int32)
    large = jnp.minimum(large, N_BUCKETS - 1)
    return jnp.where(n < max_exact, n, large)


def masked_softmax(logits, valid):
    p = jax.nn.softmax(jnp.where(valid, logits, NEG_INF), axis=-1)
    return p * valid


def compress(kv, pos, w1, b1, w2):
    B, S, G, dh = kv.shape
    chunks = kv.reshape(B, S // CMP_STRIDE, CMP_STRIDE, G, dh)
    blocks = jnp.concatenate([chunks[:, :-1], chunks[:, 1:]], axis=2)
    blocks = blocks + pos[None, None, :, None, :]
    nc = blocks.shape[1]
    flat = blocks.transpose(0, 1, 3, 2, 4).reshape(B, nc, G, CMP_BLOCK * dh)
    hid = jax.nn.gelu(flat @ w1 + b1)
    return hid @ w2


def selection_overlap(n_cmp, n_slc):
    i = jnp.arange(n_cmp)[:, None]
    j = jnp.arange(n_slc)[None, :]
    ratio = SLC_BLOCK // CMP_STRIDE
    diff = i - ratio * j
    w = jnp.zeros((n_cmp, n_slc), jnp.float32)
    for n in range(CMP_BLOCK // CMP_STRIDE):
        w = w + ((diff + n >= 0) & (diff + n < ratio)).astype(jnp.float32)
    return w


def nsa_mixer(h, w_in, pos_k, pos_v, ck_w1, ck_b1, ck_w2, cv_w1, cv_b1, cv_w2, w_out, rel_bias):
    B, S, _ = h.shape
    G, HG = N_KV_GROUPS, HEADS_PER_GROUP
    proj = h @ w_in
    offs = []
    acc = 0
    for wdt in NSA_SPLITS[:-1]:
        acc += wdt
        offs.append(acc)
    q, k_c, v_c, k_s, v_s, k_w, v_w, g = jnp.split(proj, offs, axis=-1)
    q = q.reshape(B, S, G, HG, D_K) * (D_K ** -0.5)
    gates = jax.nn.sigmoid(g).reshape(B, S, 3, G, HG)
    k_cmp = compress(k_c.reshape(B, S, G, D_K), pos_k, ck_w1, ck_b1, ck_w2)
    v_cmp = compress(v_c.reshape(B, S, G, D_V), pos_v, cv_w1, cv_b1, cv_w2)
    n_cmp = k_cmp.shape[1]
    n_slc = S // SLC_BLOCK
    n_sel = min(N_SLC, n_slc)
    cmp_end = jnp.arange(n_cmp) * CMP_STRIDE + CMP_BLOCK - 1
    overlap = selection_overlap(n_cmp, n_slc)
    k_sb = k_s.reshape(B, n_slc, SLC_BLOCK, G, D_K).transpose(0, 3, 1, 2, 4)
    v_sb = v_s.reshape(B, n_slc, SLC_BLOCK, G, D_V).transpose(0, 3, 1, 2, 4)
    k_wp = jnp.pad(k_w.reshape(B, S, G, D_K), ((0, 0), (WINDOW, 0), (0, 0), (0, 0)))
    v_wp = jnp.pad(v_w.reshape(B, S, G, D_V), ((0, 0), (WINDOW, 0), (0, 0), (0, 0)))
    tbl = rel_bias.astype(jnp.float32).reshape(N_BUCKETS, G, HG)
    tbl_g = tbl.transpose(1, 0, 2)
    b_ix = jnp.arange(B)[:, None, None, None]
    g_ix = jnp.arange(G)[None, :, None, None]
    j_ix = jnp.arange(n_slc)

    def block(qb):
        q0 = qb * Q_BLOCK
        t = q0 + jnp.arange(Q_BLOCK)
        qblk = lax.dynamic_slice_in_dim(q, q0, Q_BLOCK, axis=1)
        gblk = lax.dynamic_slice_in_dim(gates, q0, Q_BLOCK, axis=1)
        dist_c = t[:, None] - cmp_end[None, :]
        valid_c = dist_c >= 0
        logit_c = (jnp.einsum('bqghd,bngd->bghqn', qblk, k_cmp).astype(jnp.float32)
                   + tbl[rel_bucket(dist_c)].transpose(2, 3, 0, 1))
        p_c = masked_softmax(logit_c, valid_c)
        o_c = jnp.einsum('bghqn,bngd->bqghd', p_c.astype(v_cmp.dtype), v_cmp)
        imp = jnp.einsum('bghqn,nj->bgqj', p_c, overlap)
        cur = (t // SLC_BLOCK)[:, None]
        forced = (j_ix[None, :] == 0) | (j_ix[None, :] == cur) | (j_ix[None, :] == cur - 1)
        imp = jnp.where(forced, FORCE_SCORE, imp)
        imp = jnp.where(j_ix[None, :] > cur, -1.0, imp)
        _, idx = lax.top_k(imp, n_sel)
        k_sel = k_sb[b_ix, g_ix, idx]
        v_sel = v_sb[b_ix, g_ix, idx]
        pos_s = idx[..., None] * SLC_BLOCK + jnp.arange(SLC_BLOCK)
        dist_s = t[None, None, :, None, None] - pos_s
        valid_s = (dist_s >= 0).reshape(B, G, 1, Q_BLOCK, n_sel * SLC_BLOCK)
        bias_s = tbl_g[g_ix[..., None], rel_bucket(dist_s)].transpose(0, 1, 5, 2, 3, 4)
        logit_s = (jnp.einsum('bqghd,bgqktd->bghqkt', qblk, k_sel).astype(jnp.float32) + bias_s)
        logit_s = logit_s.reshape(B, G, HG, Q_BLOCK, n_sel * SLC_BLOCK)
        p_s = masked_softmax(logit_s, valid_s)
        o_s = jnp.einsum('bghqn,bgqnd->bqghd', p_s.astype(v_sel.dtype),
                         v_sel.reshape(B, G, Q_BLOCK, n_sel * SLC_BLOCK, D_V))
        k_blk = lax.dynamic_slice_in_dim(k_wp, q0, Q_BLOCK + WINDOW, axis=1)
        v_blk = lax.dynamic_slice_in_dim(v_wp, q0, Q_BLOCK + WINDOW, axis=1)
        s = q0 - WINDOW + jnp.arange(Q_BLOCK + WINDOW)
        dist_w = t[:, None] - s[None, :]
        valid_w = (s[None, :] >= 0) & (dist_w >= 0) & (dist_w < WINDOW)
        logit_w = (jnp.einsum('bqghd,bkgd->bghqk', qblk, k_blk).astype(jnp.float32)
                   + tbl[rel_bucket(dist_w)].transpose(2, 3, 0, 1))
        p_w = masked_softmax(logit_w, valid_w)
        o_w = jnp.einsum('bghqk,bkgd->bqghd', p_w.astype(v_blk.dtype), v_blk)
        return (gblk[:, :, 0, :, :, None] * o_c + gblk[:, :, 1, :, :, None] * o_s
                + gblk[:, :, 2, :, :, None] * o_w)

    outs = lax.map(block, jnp.arange(S // Q_BLOCK))
    o = outs.transpose(1, 0, 2, 3, 4, 5).reshape(B, S, N_HEADS * D_V)
    return o @ w_out


def pool_mixer(h, w_in, w_grp, scale, w_out):
    B, S, D = h.shape
    u = (h @ w_in).astype(jnp.float32)
    cs = jnp.pad(jnp.cumsum(u, axis=1), ((0, 0), (1, 0), (0, 0)))
    t = jnp.arange(S)
    groups = []
    for gi, w in enumerate(POOL_WINDOWS):
        lo, hi = gi * POOL_GROUP, (gi + 1) * POOL_GROUP
        start = jnp.maximum(t + 1 - w, 0)
        total = cs[:, 1:, lo:hi] - cs[:, start, lo:hi]
        cnt = (t + 1 - start).astype(jnp.float32)
        groups.append(total / cnt[None, :, None] - u[:, :, lo:hi])
    pooled = jnp.stack(groups, axis=2)
    mixed = jnp.einsum('bsgc,gcd->bsgd', pooled, w_grp.astype(jnp.float32)).reshape(B, S, D)
    mixed = (mixed * scale.astype(jnp.float32)).astype(h.dtype)
    return mixed @ w_out


def swiglu(h, w_gu, w_down):
    gu = h @ w_gu
    gate, up = gu[..., :D_FF], gu[..., D_FF:]
    return (jax.nn.silu(gate) * up) @ w_down


def _normal(key, shape, scale):
    return jax.random.normal(key, shape, jnp.float32) * scale


def setup_inputs(seed: int = 0) -> dict:
    key = jax.random.key(seed)
    ks = jax.random.split(key, 24)
    A, P = N_NSA_LAYERS, N_POOL_LAYERS
    return {
        'x': _normal(ks[0], (BATCH, SEQ, D_MODEL), 1.0),
        'norm_mix': 1.0 + _normal(ks[1], (DEPTH, D_MODEL), 0.02),
        'norm_ffn': 1.0 + _normal(ks[2], (DEPTH, D_MODEL), 0.02),
        'norm_final': 1.0 + _normal(ks[3], (D_MODEL,), 0.02),
        'rel_bias': _normal(ks[4], (N_BUCKETS, N_HEADS), 0.2),
        'nsa_w_in': _normal(ks[5], (A, D_MODEL, NSA_PROJ), D_MODEL ** -0.5),
        'nsa_pos_k': _normal(ks[6], (A, CMP_BLOCK, D_K), 0.1),
        'nsa_pos_v': _normal(ks[7], (A, CMP_BLOCK, D_V), 0.1),
        'nsa_ck_w1': _normal(ks[8], (A, CMP_BLOCK * D_K, CMP_HIDDEN), (CMP_BLOCK * D_K) ** -0.5),
        'nsa_ck_b1': _normal(ks[9], (A, CMP_HIDDEN), 0.01),
        'nsa_ck_w2': _normal(ks[10], (A, CMP_HIDDEN, D_K), CMP_HIDDEN ** -0.5),
        'nsa_cv_w1': _normal(ks[11], (A, CMP_BLOCK * D_V, CMP_HIDDEN), (CMP_BLOCK * D_V) ** -0.5),
        'nsa_cv_b1': _normal(ks[12], (A, CMP_HIDDEN), 0.01),
        'nsa_cv_w2': _normal(ks[13], (A, CMP_HIDDEN, D_V), CMP_HIDDEN ** -0.5),
        'nsa_w_out': _normal(ks[14], (A, N_HEADS * D_V, D_MODEL), (N_HEADS * D_V) ** -0.5),
        'pool_w_in': _normal(ks[15], (P, D_MODEL, D_MODEL), D_MODEL ** -0.5),
        'pool_w_grp': _normal(ks[16], (P, N_POOL_GROUPS, POOL_GROUP, POOL_GROUP), POOL_GROUP ** -0.5),
        'pool_scale': 1.0 + _normal(ks[17], (P, D_MODEL), 0.1),
        'pool_w_out': _normal(ks[18], (P, D_MODEL, D_MODEL), D_MODEL ** -0.5),
        'ffn_w_gu': _normal(ks[19], (DEPTH, D_MODEL, 2 * D_FF), D_MODEL ** -0.5),
        'ffn_w_down': _normal(ks[20], (DEPTH, D_FF, D_MODEL), D_FF ** -0.5),
    }


def reference(x, norm_mix, norm_ffn, norm_final, rel_bias, nsa_w_in, nsa_pos_k, nsa_pos_v,
              nsa_ck_w1, nsa_ck_b1, nsa_ck_w2, nsa_cv_w1, nsa_cv_b1, nsa_cv_w2, nsa_w_out,
              pool_w_in, pool_w_grp, pool_scale, pool_w_out, ffn_w_gu, ffn_w_down):
    for i in range(DEPTH):
        h = rms_norm(x, norm_mix[i])
        li = i // 2
        if i % 2 == 0:
            x = x + nsa_mixer(h, nsa_w_in[li], nsa_pos_k[li], nsa_pos_v[li],
                              nsa_ck_w1[li], nsa_ck_b1[li], nsa_ck_w2[li],
                              nsa_cv_w1[li], nsa_cv_b1[li], nsa_cv_w2[li],
                              nsa_w_out[li], rel_bias)
        else:
            x = x + pool_mixer(h, pool_w_in[li], pool_w_grp[li], pool_scale[li], pool_w_out[li])
        h = rms_norm(x, norm_ffn[i])
        x = x + swiglu(h, ffn_w_gu[i], ffn_w_down[i])
    return rms_norm(x, norm_final)
```

```python
import math
from contextlib import ExitStack
import numpy as np
import concourse.bass as bass
import concourse.mybir as mybir
from concourse.bass_utils import run_bass_kernel_spmd


F32 = mybir.dt.float32
BF16 = mybir.dt.bfloat16
ALU = mybir.AluOpType
AF = mybir.ActivationFunctionType
AX = mybir.AxisListType

EPOCH = 30000


class Buf:
    def __init__(self, ctx, t, name):
        self.ctx = ctx
        self.t = t
        self.name = name
        self.last_write = None
        self.reads = []
        self.dma_sem = None
        self.dma_cnt = 0
        self.is_psum = False

    def __getitem__(self, idx):
        return self.t[idx]


class Ctx:
    def __init__(self, nc, stack):
        self.nc = nc
        self.stack = stack
        self.sem_stack = stack
        self.eng = {"pe": nc.tensor, "dve": nc.vector, "act": nc.scalar,
                    "pool": nc.gpsimd, "sp": nc.sync}
        self.cur_sem = {}
        self.cnt = {}
        self.waited = {e: {} for e in self.eng}
        self.nsem = 0
        self.ninst = {e: 0 for e in self.eng}
        self.nwait = 0
        self.owners = []
        self.dead = False
        for e in self.eng:
            if e == "sp":
                continue
            self.cur_sem[e] = self._newsem("s_" + e)
            self.cnt[e] = 0

    def _newsem(self, name):
        self.nsem += 1
        return self.sem_stack.enter_context(self.nc.semaphore(f"{name}_{self.nsem}"))

    def sbuf(self, name, shape, dtype):
        t = self.stack.enter_context(self.nc.sbuf_tensor(name, list(shape), dtype))
        return Buf(self, t, name)

    def psum(self, name, shape, dtype=F32):
        t = self.stack.enter_context(self.nc.psum_tensor(name, list(shape), dtype))
        b = Buf(self, t, name)
        b.is_psum = True
        return b

    def dram(self, name, shape, dtype, kind):
        t = self.nc.dram_tensor(name, list(shape), dtype, kind=kind)
        return Buf(self, t.ap(), name)

    def _wait(self, e, deps):
        eng = self.eng[e]
        w = self.waited[e]
        best = {}
        for d in deps:
            if d is None:
                continue
            sem, val, src = d
            if src == "pe" and e == "pe":
                continue
            k = id(sem)
            if w.get(k, 0) >= val:
                continue
            if k not in best or best[k][1] < val:
                best[k] = (sem, val)
        for k, (sem, val) in best.items():
            eng.wait_ge(sem, val)
            w[k] = val
            self.nwait += 1

    def _deps(self, reads, writes):
        deps = []
        for b in reads:
            deps.append(b.last_write)
        for b in writes:
            deps.append(b.last_write)
            deps.extend(b.reads)
        return deps

    def _commit(self, ev, reads, writes):
        for b in reads:
            b.reads.append(ev)
            if len(b.reads) > 64:
                best = {}
                for s, v, src in b.reads:
                    if id(s) not in best or best[id(s)][1] < v:
                        best[id(s)] = (s, v, src)
                b.reads = list(best.values())
        for b in writes:
            b.last_write = ev
            b.reads = []

    def op(self, e, fn, reads=(), writes=()):
        if self.dead:
            return None
        deps = self._deps(reads, writes)
        for b in reads:
            if b.is_psum:
                deps.extend(r for r in b.reads if r[2] != e)
        self._wait(e, deps)
        ins = fn(self.eng[e])
        if self.cnt[e] >= EPOCH:
            self.cur_sem[e] = self._newsem("s_" + e)
            self.cnt[e] = 0
        self.cnt[e] += 1
        sem = self.cur_sem[e]
        ins.then_inc(sem, 1)
        ev = (sem, self.cnt[e], e)
        self._commit(ev, reads, writes)
        self.ninst[e] += 1
        return ev

    def dma(self, q, out_ap, in_ap, owner, reads=(), writes=(), **kw):
        if self.dead:
            return None
        self._wait(q, self._deps(reads, writes))
        if owner.dma_sem is None:
            owner.dma_sem = self._newsem("d_" + owner.name)
            self.owners.append(owner)
        ins = self.eng[q].dma_start(out=out_ap, in_=in_ap, **kw)
        owner.dma_cnt += 16
        ins.then_inc(owner.dma_sem, 16)
        ev = (owner.dma_sem, owner.dma_cnt, "dma")
        self._commit(ev, reads, writes)
        self.ninst[q] += 1
        return ev

    def finish(self, e, bufs):
        if self.dead:
            return
        deps = []
        for b in bufs:
            deps.append(b.last_write)
            deps.extend(b.reads)
        self._wait(e, deps)

    def barrier(self):
        if self.dead:
            return
        deps = [(self.cur_sem[e], self.cnt[e], e) for e in self.cur_sem if self.cnt[e] > 0]
        deps += [(b.dma_sem, b.dma_cnt, "dma") for b in self.owners]
        for q in self.eng:
            self._wait(q, [d for d in deps if not (d[2] == q)])


S = 16384
D = 2048
NCOL = 1272
FM_COLS = 1120
TM_COLS = 152
EPS = 1e-6
NEG = -30000.0


class _Stop(Exception):
    pass


def build_attn(nqt=S // 512, stop=None):
    nc = bass.Bass("TRN2", target_bir_lowering=False)
    with ExitStack() as st:
      try:
        _build_attn(nc, st, nqt, stop)
      except _Stop:
        pass
    return nc


def _build_attn(nc, st, nqt, stop):
    if True:
        c = Ctx(nc, st)
        IN = "ExternalInput"
        xT = c.dram("xT", [D, S], F32, IN)
        Wg = c.dram("Wg", [D, NCOL], F32, IN)
        gmix = c.dram("gmix", [128, 16], F32, IN)
        ck_w1 = c.dram("ck_w1", [3072, 256], F32, IN)
        cv_w1 = c.dram("cv_w1", [2048, 256], F32, IN)
        ck_w2 = c.dram("ck_w2", [256, 96], F32, IN)
        cv_w2 = c.dram("cv_w2", [256, 64], F32, IN)
        posT_k = c.dram("posT_k", [96, 32], F32, IN)
        posT_v = c.dram("posT_v", [64, 32], F32, IN)
        b1k_d = c.dram("b1k", [128, 2], F32, IN)
        b1v_d = c.dram("b1v", [128, 2], F32, IN)
        patC_d = c.dram("patC", [64, 8, 512], F32, IN)
        patW_d = c.dram("patW", [128, 8, 640], F32, IN)
        patS_d = c.dram("patS", [5, 8, 128, 512], F32, IN)
        bias31_d = c.dram("bias31", [128, 8], F32, IN)
        AW_d = c.dram("AW", [128, 512], F32, IN)
        BW_d = c.dram("BW", [128, 512], F32, IN)
        E32_d = c.dram("E32", [128, 4096], F32, IN)
        OVW_d = c.dram("OVW", [64, 512], F32, IN)
        ov_d = c.dram("ov", [1024, 256], F32, IN)
        ident_d = c.dram("ident", [128, 128], F32, IN)
        rowmask_d = c.dram("rowmask", [64, 1], F32, IN)
        o_d = c.dram("o", [S, 512], F32, "ExternalOutput")
        q_d = c.dram("q_scr", [8, 96, S], BF16, "ExternalOutput")
        ksT_d = c.dram("ks_scr", [96, S], BF16, "ExternalOutput")
        kwT_d = c.dram("kw_scr", [96, S], BF16, "ExternalOutput")
        vsw_d = c.dram("vsw_scr", [S, 128], BF16, "ExternalOutput")
        gates_d = c.dram("gates_scr", [S, 24], F32, "ExternalOutput")
        scratch_events = []

        ps = [c.psum(f"ps{i}", [128, 512]) for i in range(8)]
        pstate = {"i": 0, "lo": 0, "n": 8}

        def next_ps():
            p = ps[pstate["lo"] + pstate["i"] % pstate["n"]]
            pstate["i"] += 1
            return p

        kcT = c.sbuf("kcT", [96, 32 + 1024], BF16)
        vcat = c.sbuf("vcat", [128, 8, 321], BF16)
        vcB = c.sbuf("vcB", [64, 32, 65], BF16)
        ones = c.sbuf("ones", [128, 128], BF16)
        one32 = c.sbuf("one32", [1, 1], F32)
        epsb = c.sbuf("epsb", [128, 1], F32)
        ident = c.sbuf("ident_sb", [128, 128], F32)
        c.op("pool", lambda e: e.memset(ones[:], 1.0), writes=[ones])
        c.op("pool", lambda e: e.memset(one32[:], 1.0), writes=[one32])
        c.op("pool", lambda e: e.memset(epsb[:], EPS), writes=[epsb])
        c.op("pool", lambda e: e.memset(kcT[:], 0.0), writes=[kcT])
        c.op("pool", lambda e: e.memset(vcat[:], 0.0), writes=[vcat])
        c.op("pool", lambda e: e.memset(vcB[:], 0.0), writes=[vcB])
        c.dma("sp", ident[:], ident_d[:], ident, reads=[], writes=[ident])

        with ExitStack() as sB:
            kTc_t = sB.enter_context(nc.sbuf_tensor("kTc", [96, 16, 1024], BF16))
            vTc_t = sB.enter_context(nc.sbuf_tensor("vTc", [64, 16, 1024], BF16))
            kTc = Buf(c, kTc_t, "kTc")
            vTc = Buf(c, vTc_t, "vTc")
            c.op("pool", lambda e: e.memset(kTc_t[:], 0.0), writes=[kTc])
            c.op("pool", lambda e: e.memset(vTc_t[:], 0.0), writes=[vTc])
            with ExitStack() as sA:
                c.stack = sA
                Wsb = c.sbuf("Wsb", [128, 16, NCOL], BF16)
                gm = c.sbuf("gm", [128, 16], F32)
                c.dma("sp", gm[:], gmix[:], gm, writes=[gm])
                wst = [c.sbuf(f"wst{i}", [128, 16, 128], F32) for i in range(2)]
                nblk = (NCOL + 127) // 128
                for bi in range(nblk):
                    c0 = bi * 128
                    wd = min(128, NCOL - c0)
                    w = wst[bi % 2]
                    c.dma("sp", w[:, :, :wd], Wg.t[:, c0:c0 + wd].rearrange("(k p) m -> p k m", p=128), w, writes=[w])
                    for k in range(16):
                        c.op("dve" if k % 2 == 0 else "pool",
                             lambda e: e.tensor_scalar(out=Wsb[:, k, c0:c0 + wd], in0=w[:, k, :wd], scalar1=gm[:, k:k + 1],
                                                       scalar2=None, op0=ALU.mult), reads=[w, gm], writes=[Wsb])
                if stop == 'p1w':
                    c.barrier(); c.dead = True
                xb = [c.sbuf(f"xb{i}", [128, 16, 512], BF16) for i in range(2)]
                sq = [c.sbuf(f"sq{i}", [128, 512], BF16) for i in range(2)]
                rstd = c.sbuf("rstd", [128, 512], F32)
                rcol = c.sbuf("rcol", [128, 4], F32)
                qst = [c.sbuf(f"qst{i}", [96, 8, 512], BF16) for i in range(2)]
                kst = [c.sbuf(f"kst{i}", [96, 512], BF16) for i in range(2)]
                kwt = [c.sbuf(f"kwt{i}", [96, 512], BF16) for i in range(2)]
                vst = [c.sbuf(f"vst{i}", [128, 4, 128], BF16) for i in range(2)]
                gst = [c.sbuf(f"gst{i}", [128, 4, 24], F32) for i in range(2)]
                for ti in range(nqt):
                    T0 = ti * 512
                    x = xb[ti % 2]
                    c.dma("pool", x[:], xT.t[:, T0:T0 + 512].rearrange("(k p) t -> p k t", p=128), x, writes=[x])
                    pss = next_ps()
                    for k in range(16):
                        s_ = sq[k % 2]
                        c.op("act", lambda e: e.activation(out=s_[:], in_=x[:, k, :], func=AF.Square), reads=[x], writes=[s_])
                        c.op("pe", lambda e: e.matmul(pss[:], lhsT=ones[:], rhs=s_[:], start=(k == 0), stop=(k == 15)),
                             reads=[ones, s_], writes=[pss])
                    c.op("act", lambda e: e.activation(out=rstd[:], in_=pss[:], func=AF.Sqrt, bias=epsb[:, 0:1], scale=1.0 / D),
                         reads=[pss, epsb], writes=[rstd])
                    c.op("dve", lambda e: e.reciprocal(out=rstd[:], in_=rstd[:]), reads=[rstd], writes=[rstd])
                    pc = next_ps()
                    for s in range(4):
                        c.op("pe", lambda e: e.matmul(pc[:, s:s + 1], lhsT=rstd[0:1, s * 128:(s + 1) * 128], rhs=one32[0:1, 0:1],
                                                      start=True, stop=True), reads=[rstd, one32], writes=[pc])
                    c.op("act", lambda e: e.activation(out=rcol[:], in_=pc[:, 0:4], func=AF.Copy), reads=[pc], writes=[rcol])
                    if stop == 'p1s':
                        c.barrier(); c.dead = True
                    qs, ks_, kw_ = qst[ti % 2], kst[ti % 2], kwt[ti % 2]
                    groups = [(h * 96, 96, ("q", h)) for h in range(8)] + [(768, 96, ("kc", 0)), (864, 64, ("vc", 0)),
                                                                          (928, 96, ("ks", 0)), (1024, 96, ("kw", 0))]
                    for (c0, wd, (kind, h)) in groups:
                        p = next_ps()
                        for k in range(16):
                            c.op("pe", lambda e: e.matmul(p[0:wd, :], lhsT=Wsb[:, k, c0:c0 + wd], rhs=x[:, k, :],
                                                          start=(k == 0), stop=(k == 15)), reads=[Wsb, x], writes=[p])
                        if kind == "q":
                            c.op("dve", lambda e: e.scalar_tensor_tensor(out=qs[0:96, h, :], in0=p[0:96, :], scalar=96.0 ** -0.5,
                                                                        in1=rstd[0:96, :], op0=ALU.mult, op1=ALU.mult),
                                 reads=[p, rstd], writes=[qs])
                        elif kind in ("kc", "vc"):
                            dst_t, dstB = (kTc_t, kTc) if kind == "kc" else (vTc_t, vTc)
                            ch0 = ti * 32
                            c.op("dve", lambda e: e.tensor_tensor(
                                out=dst_t[0:wd, :, ch0:ch0 + 32].rearrange("p r c -> p c r"),
                                in0=p[0:wd, :].rearrange("p (c r) -> p c r", r=16),
                                in1=rstd[0:wd, :].rearrange("p (c r) -> p c r", r=16), op=ALU.mult),
                                reads=[p, rstd], writes=[dstB])
                        else:
                            dst = ks_ if kind == "ks" else kw_
                            c.op("dve", lambda e: e.tensor_tensor(out=dst[0:96, :], in0=p[0:96, :], in1=rstd[0:96, :], op=ALU.mult),
                                 reads=[p, rstd], writes=[dst])
                    if stop == 'p1f':
                        c.barrier(); c.dead = True
                    vs_, gs_ = vst[ti % 2], gst[ti % 2]
                    for s in range(4):
                        p = next_ps()
                        for k in range(16):
                            c.op("pe", lambda e: e.matmul(p[:, 0:TM_COLS], lhsT=x[:, k, s * 128:(s + 1) * 128], rhs=Wsb[:, k, FM_COLS:NCOL],
                                                          start=(k == 0), stop=(k == 15)), reads=[Wsb, x], writes=[p])
                        c.op("dve", lambda e: e.tensor_scalar(out=vs_[:, s, :], in0=p[:, 0:128], scalar1=rcol[:, s:s + 1], scalar2=None,
                                                             op0=ALU.mult), reads=[p, rcol], writes=[vs_])
                        c.op("act", lambda e: e.activation(out=gs_[:, s, :], in_=p[:, 128:152], func=AF.Sigmoid, scale=rcol[:, s:s + 1]),
                             reads=[p, rcol, vs_], writes=[gs_])
                    if stop == 'p1t':
                        c.barrier(); c.dead = True
                    scratch_events.append(c.dma("sp", q_d.t[:, :, T0:T0 + 512].rearrange("h d t -> d h t"), qs[:], qs, reads=[qs]))
                    scratch_events.append(c.dma("sp", ksT_d.t[:, T0:T0 + 512], ks_[:], ks_, reads=[ks_]))
                    scratch_events.append(c.dma("sp", kwT_d.t[:, T0:T0 + 512], kw_[:], kw_, reads=[kw_]))
                    scratch_events.append(c.dma("sp", vsw_d.t[T0:T0 + 512, :].rearrange("(s p) c -> p s c", p=128), vs_[:], vs_, reads=[vs_]))
                    scratch_events.append(c.dma("sp", gates_d.t[T0:T0 + 512, :].rearrange("(s p) c -> p s c", p=128), gs_[:], gs_, reads=[gs_]))
                c.barrier()
                if stop == 'p1':
                    c.dead = True
                c.stack = st
            with ExitStack() as sC:
                c.stack = sC
                for (nm, dk, w1_d, w2_d, posT_d, b1_d, src_t, srcB) in (("k", 96, ck_w1, ck_w2, posT_k, b1k_d, kTc_t, kTc),
                                                                        ("v", 64, cv_w1, cv_w2, posT_v, b1v_d, vTc_t, vTc)):
                    with ExitStack() as sD:
                        c.stack = sD
                        W1 = c.sbuf("W1" + nm, [dk, 32, 256], BF16)
                        c.dma("pool", W1[:], w1_d.t.rearrange("(l d) j -> d l j", d=dk), W1, writes=[W1])
                        W2 = c.sbuf("W2" + nm, [128, 2, dk], BF16)
                        c.dma("pool", W2[:], w2_d.t.rearrange("(jh p) d -> p jh d", p=128), W2, writes=[W2])
                        pos = c.sbuf("pos" + nm, [dk, 32], BF16)
                        c.dma("pool", pos[:], posT_d[:], pos, writes=[pos])
                        b1 = c.sbuf("b1s" + nm, [128, 2], F32)
                        c.dma("sp", b1[:], b1_d[:], b1, writes=[b1])
                        cst = c.sbuf("cst" + nm, [128, 2], F32)
                        pcst = next_ps()
                        for jh in range(2):
                            for l in range(32):
                                c.op("pe", lambda e: e.matmul(pcst[:, jh:jh + 1], lhsT=W1[:, l, jh * 128:(jh + 1) * 128], rhs=pos[:, l:l + 1],
                                                              start=(l == 0), stop=(l == 31)), reads=[W1, pos], writes=[pcst])
                        c.op("dve", lambda e: e.tensor_tensor(out=cst[:], in0=pcst[:, 0:2], in1=b1[:], op=ALU.add), reads=[pcst, b1], writes=[cst])
                        hid = c.sbuf("hid" + nm, [128, 2, 32 + 1024], BF16)
                        c.op("pool", lambda e: e.memset(hid[:], 0.0), writes=[hid])
                        xh = c.sbuf("xh" + nm, [128, 512], F32)
                        tt_ = c.sbuf("tt" + nm, [128, 512], F32)
                        for jh in range(2):
                            for nb in range(2):
                                N = 512 if nb == 0 else 511
                                p = next_ps()
                                for l in range(32):
                                    a0 = nb * 512 + (l // 16)
                                    c.op("pe", lambda e: e.matmul(p[:, 0:N], lhsT=W1[:, l, jh * 128:(jh + 1) * 128], rhs=src_t[0:dk, l % 16, a0:a0 + N],
                                                                  start=(l == 0), stop=(l == 31)), reads=[W1, srcB], writes=[p])
                                c.op("act", lambda e: e.activation(out=xh[:, 0:N], in_=p[:, 0:N], func=AF.Identity, bias=cst[:, jh:jh + 1]),
                                     reads=[p, cst], writes=[xh])
                                c.op("dve", lambda e: e.tensor_tensor(out=tt_[:, 0:N], in0=xh[:, 0:N], in1=xh[:, 0:N], op=ALU.mult), reads=[xh], writes=[tt_])
                                c.op("dve", lambda e: e.tensor_scalar(out=tt_[:, 0:N], in0=tt_[:, 0:N], scalar1=0.044715, scalar2=1.0, op0=ALU.mult, op1=ALU.add),
                                     reads=[tt_], writes=[tt_])
                                c.op("dve", lambda e: e.tensor_tensor(out=tt_[:, 0:N], in0=tt_[:, 0:N], in1=xh[:, 0:N], op=ALU.mult), reads=[tt_, xh], writes=[tt_])
                                c.op("act", lambda e: e.activation(out=tt_[:, 0:N], in_=tt_[:, 0:N], func=AF.Sigmoid, scale=2.0 * math.sqrt(2.0 / math.pi)),
                                     reads=[tt_], writes=[tt_])
                                c.op("dve", lambda e: e.tensor_tensor(out=hid[:, jh, 32 + nb * 512:32 + nb * 512 + N], in0=tt_[:, 0:N], in1=xh[:, 0:N], op=ALU.mult),
                                     reads=[tt_, xh], writes=[hid])
                        if nm == "k":
                            for nb in range(2):
                                p = next_ps()
                                for jh in range(2):
                                    c.op("pe", lambda e: e.matmul(p[0:96, :], lhsT=W2[:, jh, :], rhs=hid[:, jh, 32 + nb * 512:32 + (nb + 1) * 512],
                                                                  start=(jh == 0), stop=(jh == 1)), reads=[W2, hid], writes=[p])
                                c.op("act", lambda e: e.activation(out=kcT[0:96, 32 + nb * 512:32 + (nb + 1) * 512], in_=p[0:96, :], func=AF.Copy),
                                     reads=[p], writes=[kcT])
                        else:
                            for i in range(8):
                                p = next_ps()
                                for jh in range(2):
                                    c.op("pe", lambda e: e.matmul(p[:, 0:64], lhsT=hid[:, jh, 32 + i * 128:32 + (i + 1) * 128], rhs=W2[:, jh, :],
                                                                  start=(jh == 0), stop=(jh == 1)), reads=[W2, hid], writes=[p])
                                c.op("act", lambda e: e.activation(out=vcat[:, i, 0:64], in_=p[:, 0:64], func=AF.Copy), reads=[p], writes=[vcat])
                            for qi in range(32):
                                p = next_ps()
                                for jh in range(2):
                                    c.op("pe", lambda e: e.matmul(p[0:64, 0:64], lhsT=hid[:, jh, 32 * qi:32 * qi + 64], rhs=W2[:, jh, :],
                                                                  start=(jh == 0), stop=(jh == 1)), reads=[W2, hid], writes=[p])
                                c.op("act", lambda e: e.activation(out=vcB[0:64, qi, 0:64], in_=p[0:64, 0:64], func=AF.Copy), reads=[p], writes=[vcB])
                        c.barrier()
                        c.stack = sC
                c.stack = st
        if stop == 'p1b':
            c.barrier()
            c.dead = True
        c.op("pool", lambda e: e.memset(vcat[:, :, 64:65], 1.0), writes=[vcat])
        c.op("pool", lambda e: e.memset(vcB[:, :, 64:65], 1.0), writes=[vcB])
        ovl = Buf(c, None, "ovl")
        c.dma("pool", vcat[:, :, 65:321], ov_d.t.rearrange("(i p) j -> p i j", p=128), ovl, writes=[vcat, ovl])

        if not c.dead:
            for q in ("sp", "pool", "act"):
                c._wait(q, scratch_events)
        ksT = c.sbuf("ksT", [96, S], BF16)
        vs = c.sbuf("vs", [128, S // 128, 65], BF16)
        c.op("pool", lambda e: e.memset(vs[:, :, 64:65], 1.0), writes=[vs])
        Sx = nqt * 512
        nch = (Sx + 4095) // 4096
        for i in range(nch):
            c1 = min(Sx, (i + 1) * 4096)
            c.dma("sp", ksT[:, i * 4096:c1], ksT_d.t[:, i * 4096:c1], Buf(c, None, f"ksl{i}"), writes=[ksT])
        nch = (Sx + 2047) // 2048
        for i in range(nch):
            r1 = min(Sx, (i + 1) * 2048)
            c.dma("sp", vs[:, i * 16:r1 // 128, 0:64], vsw_d.t[i * 2048:r1, 0:64].rearrange("(k p) d -> p k d", p=128),
                  Buf(c, None, f"vsl{i}"), writes=[vs])
        stage = c.sbuf("stage", [128, 640], F32)
        patC = c.sbuf("patC_sb", [64, 8, 512], BF16)
        patW = c.sbuf("patW_sb", [128, 8, 640], BF16)
        for h in range(8):
            c.dma("sp", stage[0:64, 0:512], patC_d.t[:, h, :], stage, writes=[stage])
            c.op("act", lambda e: e.activation(out=patC[:, h, :], in_=stage[0:64, 0:512], func=AF.Exp), reads=[stage], writes=[patC])
            c.dma("sp", stage[:, :], patW_d.t[:, h, :], stage, writes=[stage])
            c.op("act", lambda e: e.activation(out=patW[:, h, :], in_=stage[:, :], func=AF.Exp), reads=[stage], writes=[patW])
        bias31 = c.sbuf("bias31_sb", [128, 8], F32)
        c.dma("sp", bias31[:], bias31_d[:], bias31, writes=[bias31])
        AW = c.sbuf("AW_sb", [128, 512], F32)
        BW = c.sbuf("BW_sb", [128, 512], F32)
        c.dma("sp", AW[:], AW_d[:], AW, writes=[AW])
        c.dma("sp", BW[:], BW_d[:], BW, writes=[BW])
        E32 = c.sbuf("E32_sb", [128, 4096], BF16)
        c.dma("pool", E32[:], E32_d[:], E32, writes=[E32])
        OVW = c.sbuf("OVW_sb", [64, 512], BF16)
        c.dma("pool", OVW[:], OVW_d[:], OVW, writes=[OVW])
        rowmask = c.sbuf("rowmask_sb", [64, 1], F32)
        c.dma("sp", rowmask[:], rowmask_d[:], rowmask, writes=[rowmask])

        if stop == 'p2setup':
            c.barrier()
            c.dead = True
        qT = c.sbuf("qT", [96, 8, 512], BF16)
        kwR = c.sbuf("kwR", [96, 3, 512], BF16)
        vwR = c.sbuf("vwR", [128, 3, 4, 65], BF16)
        c.op("pool", lambda e: e.memset(vwR[:, :, :, 64:65], 1.0), writes=[vwR])
        gates = c.sbuf("gates_sb", [128, 4, 24], F32)
        Pc = [c.sbuf(f"Pc{i}", [128, 512], BF16) for i in range(8)]
        Pb = c.sbuf("Pb", [64, 512], BF16)
        selT = [c.sbuf(f"selT{i}", [128, 512], BF16) for i in range(2)]
        mT = [c.sbuf(f"mT{i}", [128, 512], BF16) for i in range(2)]
        Pt = [c.sbuf(f"Pt{i}", [128, 512], BF16) for i in range(3)]
        pbias = [c.sbuf(f"pbias{i}", [128, 512], F32) for i in range(3)]
        scb = [c.sbuf(f"scb{i}", [128, 512], F32) for i in range(2)]
        Pw = [c.sbuf(f"Pw{i}", [128, 640], BF16) for i in range(2)]
        o_acc = c.sbuf("o_acc", [128, 4, 8, 64], F32)
        imp = c.sbuf("imp", [128, 4, 256], F32)
        osT = [c.sbuf(f"osT{i}", [65, 512], F32) for i in range(2)]
        impm = c.sbuf("impm", [128, 256], F32)
        wk2 = c.sbuf("wk2", [128, 256], F32)
        sel = c.sbuf("sel", [128, 256], F32)
        m1 = c.sbuf("m1", [128, 8], F32)
        m2 = c.sbuf("m2", [128, 8], F32)
        rden = [c.sbuf(f"rden{i}", [128, 1], F32) for i in range(4)]
        scl = [c.sbuf(f"scl{i}", [128, 1], F32) for i in range(4)]
        cnt = {"r": 0, "pt": 0, "pb": 0, "scb": 0, "mt": 0, "pw": 0, "os": 0}

        def rot(lst, key):
            v = lst[cnt[key] % len(lst)]
            cnt[key] += 1
            return v

        def finish_branch(pso, dcol, s, h, gidx, first):
            rd, sc = rot(rden, "r"), scl[(cnt["r"] - 1) % 4]
            c.op("dve", lambda e: e.tensor_scalar(out=rd[:], in0=pso[:, dcol:dcol + 1], scalar1=1e-30, scalar2=None, op0=ALU.add),
                 reads=[pso], writes=[rd])
            c.op("dve", lambda e: e.reciprocal(out=rd[:], in_=rd[:]), reads=[rd], writes=[rd])
            c.op("dve", lambda e: e.tensor_tensor(out=sc[:], in0=rd[:], in1=gates[:, s, gidx:gidx + 1], op=ALU.mult), reads=[rd, gates], writes=[sc])
            if first:
                c.op("act", lambda e: e.activation(out=o_acc[:, s, h, :], in_=pso[:, 0:64], func=AF.Copy, scale=sc[:, 0:1]),
                     reads=[pso, sc], writes=[o_acc])
            else:
                c.op("dve", lambda e: e.scalar_tensor_tensor(out=o_acc[:, s, h, :], in0=pso[:, 0:64], scalar=sc[:, 0:1], in1=o_acc[:, s, h, :],
                                                            op0=ALU.mult, op1=ALU.add), reads=[pso, sc, o_acc], writes=[o_acc])
            return rd

        for Qi in range(nqt):
            Q0 = Qi * 512
            slot = Qi % 3
            c.dma("sp", qT[:], q_d.t[:, :, Q0:Q0 + 512].rearrange("h d t -> d h t"), qT, writes=[qT])
            kwl = Buf(c, None, "kwl")
            c.dma("sp", kwR[:, slot, :], kwT_d.t[:, Q0:Q0 + 512], kwR, writes=[kwR])
            vwl = Buf(c, None, "vwl")
            c.dma("sp", vwR[:, slot, :, 0:64], vsw_d.t[Q0:Q0 + 512, 64:128].rearrange("(s p) d -> p s d", p=128), vwR, writes=[vwR])
            c.dma("sp", gates[:], gates_d.t[Q0:Q0 + 512, :].rearrange("(s p) g -> p s g", p=128), gates, writes=[gates])

            NF = max(0, 32 * Qi - 32)
            nfar = (NF + 127) // 128
            for h in range(8):
                for i in range(nfar):
                    rows = min(128, NF - 128 * i)
                    p = next_ps()
                    c.op("pe", lambda e: e.matmul(p[0:rows, :], lhsT=kcT[0:96, 32 + 128 * i:32 + 128 * i + rows], rhs=qT[0:96, h, :],
                                                  start=True, stop=True), reads=[kcT, qT], writes=[p])
                    c.op("act", lambda e: e.activation(out=Pc[i][0:rows, :], in_=p[0:rows, :], func=AF.Exp, bias=bias31[0:rows, h:h + 1]),
                         reads=[p, bias31], writes=[Pc[i]])
                p = next_ps()
                c.op("pe", lambda e: e.matmul(p[0:64, :], lhsT=kcT[0:96, 32 * Qi:32 * Qi + 64], rhs=qT[0:96, h, :], start=True, stop=True),
                     reads=[kcT, qT], writes=[p])
                c.op("act", lambda e: e.activation(out=Pb[:], in_=p[0:64, :], func=AF.Exp), reads=[p], writes=[Pb])
                c.op("dve", lambda e: e.tensor_tensor(out=Pb[:], in0=Pb[:], in1=patC[:, h, :], op=ALU.mult), reads=[Pb, patC], writes=[Pb])
                if Qi == 0:
                    c.op("dve", lambda e: e.tensor_scalar(out=Pb[:], in0=Pb[:], scalar1=rowmask[:, 0:1], scalar2=None, op0=ALU.mult),
                         reads=[Pb, rowmask], writes=[Pb])
                for s in range(4):
                    po, po2 = next_ps(), next_ps()
                    for i in range(nfar):
                        rows = min(128, NF - 128 * i)
                        c.op("pe", lambda e: e.matmul(po[:, 0:65], lhsT=Pc[i][0:rows, s * 128:(s + 1) * 128], rhs=vcat[0:rows, i, 0:65],
                                                      start=(i == 0), stop=False), reads=[Pc[i], vcat], writes=[po])
                    c.op("pe", lambda e: e.matmul(po[:, 0:65], lhsT=Pb[0:64, s * 128:(s + 1) * 128], rhs=vcB[0:64, Qi, :],
                                                  start=(nfar == 0), stop=True), reads=[Pb, vcB], writes=[po])
                    for i in range(nfar):
                        rows = min(128, NF - 128 * i)
                        c.op("pe", lambda e: e.matmul(po2[:, 0:256], lhsT=Pc[i][0:rows, s * 128:(s + 1) * 128], rhs=vcat[0:rows, i, 65:321],
                                                      start=(i == 0), stop=False), reads=[Pc[i], vcat], writes=[po2])
                    c.op("pe", lambda e: e.matmul(po2[:, 0:256], lhsT=Pb[0:64, s * 128:(s + 1) * 128], rhs=OVW[0:64, 256 - 8 * Qi:512 - 8 * Qi],
                                                  start=(nfar == 0), stop=True), reads=[Pb, OVW], writes=[po2])
                    rd = finish_branch(po, 64, s, h, 0 * 8 + h, True)
                    if h == 0:
                        c.op("dve", lambda e: e.tensor_scalar(out=imp[:, s, :], in0=po2[:, 0:256], scalar1=rd[:, 0:1], scalar2=None, op0=ALU.mult),
                             reads=[po2, rd], writes=[imp])
                    else:
                        c.op("dve", lambda e: e.scalar_tensor_tensor(out=imp[:, s, :], in0=po2[:, 0:256], scalar=rd[:, 0:1], in1=imp[:, s, :],
                                                                    op0=ALU.mult, op1=ALU.add), reads=[po2, rd, imp], writes=[imp])
            if stop == 'p2c':
                c.barrier()
                c.dead = True
            for s in range(4):
                qi128 = 4 * Qi + s
                lo = 256 - 2 * qi128
                c.op("dve", lambda e: e.tensor_tensor(out=impm[:], in0=imp[:, s, :], in1=AW[:, lo:lo + 256], op=ALU.mult), reads=[imp, AW], writes=[impm])
                c.op("dve", lambda e: e.tensor_tensor(out=impm[:], in0=impm[:], in1=BW[:, lo:lo + 256], op=ALU.add), reads=[impm, BW], writes=[impm])
                c.op("dve", lambda e: e.memset(impm[:, 0:1], 4.0e9), writes=[impm])
                c.op("dve", lambda e: e.max(out=m1[:], in_=impm[:]), reads=[impm], writes=[m1])
                c.op("dve", lambda e: e.match_replace(out=wk2[:], in_to_replace=m1[:], in_values=impm[:], imm_value=-2.0), reads=[m1, impm], writes=[wk2])
                c.op("dve", lambda e: e.max(out=m2[:], in_=wk2[:]), reads=[wk2], writes=[m2])
                c.op("dve", lambda e: e.tensor_scalar(out=sel[:], in0=impm[:], scalar1=m2[:, 7:8], scalar2=None, op0=ALU.is_ge), reads=[impm, m2], writes=[sel])
                for jh in range(2):
                    p = next_ps()
                    c.op("pe", lambda e: e.transpose(out=p[:, 0:128], in_=sel[:, jh * 128:(jh + 1) * 128], identity=ident[:]), reads=[sel, ident], writes=[p])
                    c.op("act", lambda e: e.activation(out=selT[jh][:, s * 128:(s + 1) * 128], in_=p[:, 0:128], func=AF.Copy), reads=[p], writes=[selT[jh]])
            if stop == 'p2sel':
                c.barrier()
                c.dead = True
            nkt = 4 * Qi + 4
            pstate["lo"], pstate["n"], pstate["i"] = 4, 4, 0
            for hg in range(2):
                for kt in range(nkt):
                    jh, a, kk = kt // 64, (kt // 32) % 2, kt % 32
                    pm = next_ps()
                    c.op("pe", lambda e: e.matmul(pm[:], lhsT=E32[64 * a:64 * a + 64, kk * 128:(kk + 1) * 128], rhs=selT[jh][64 * a:64 * a + 64, :],
                                                  start=True, stop=True), reads=[E32, selT[jh]], writes=[pm])
                    m = rot(mT, "mt")
                    c.op("dve", lambda e: e.tensor_copy(out=m[:], in_=pm[:]), reads=[pm], writes=[m])
                    near = kt >= 4 * Qi - 1
                    o = kt - (4 * Qi - 1)
                    for hh in range(4):
                        h = hg * 4 + hh
                        p = next_ps()
                        c.op("pe", lambda e: e.matmul(p[:], lhsT=ksT[0:96, kt * 128:(kt + 1) * 128], rhs=qT[0:96, h, :], start=True, stop=True),
                             reads=[ksT, qT], writes=[p])
                        P = rot(Pt, "pt")
                        if not near:
                            c.op("act", lambda e: e.activation(out=P[:], in_=p[:], func=AF.Exp, bias=bias31[:, h:h + 1]), reads=[p, bias31], writes=[P])
                        else:
                            pb = rot(pbias, "pb")
                            c.dma("sp", pb[:], patS_d.t[o, h], pb, writes=[pb])
                            sb = rot(scb, "scb")
                            c.op("dve", lambda e: e.tensor_tensor(out=sb[:], in0=p[:], in1=pb[:], op=ALU.add), reads=[p, pb], writes=[sb])
                            c.op("act", lambda e: e.activation(out=P[:], in_=sb[:], func=AF.Exp), reads=[sb], writes=[P])
                        c.op("dve", lambda e: e.tensor_tensor(out=P[:], in0=P[:], in1=m[:], op=ALU.mult), reads=[P, m], writes=[P])
                        c.op("pe", lambda e: e.matmul(ps[hh][0:65, :], lhsT=vs[:, kt, :], rhs=P[:], start=(kt == 0), stop=(kt == nkt - 1)),
                             reads=[vs, P], writes=[ps[hh]])
                for hh in range(4):
                    h = hg * 4 + hh
                    os_ = rot(osT, "os")
                    c.op("act", lambda e: e.activation(out=os_[:], in_=ps[hh][0:65, :], func=AF.Copy), reads=[ps[hh]], writes=[os_])
                    for s in range(4):
                        p = next_ps()
                        c.op("pe", lambda e: e.transpose(out=p[:, 0:65], in_=os_[0:65, s * 128:(s + 1) * 128], identity=ident[0:65, 0:65]),
                             reads=[os_, ident], writes=[p])
                        finish_branch(p, 64, s, h, 8 + h, False)
            pstate["lo"], pstate["n"], pstate["i"] = 0, 8, 0
            if stop == 'p2s':
                c.barrier()
                c.dead = True
            for s in range(4):
                qi128 = 4 * Qi + s
                omin = max(0, 4 - qi128)
                for h in range(8):
                    pA, pB = next_ps(), next_ps()
                    for o in range(omin, 5):
                        kt = qi128 - 4 + o
                        sl, sub = (kt // 4) % 3, kt % 4
                        dst = pA[:, o * 128:(o + 1) * 128] if o < 4 else pB[:, 0:128]
                        c.op("pe", lambda e: e.matmul(dst, lhsT=kwR[0:96, sl, sub * 128:(sub + 1) * 128], rhs=qT[0:96, h, s * 128:(s + 1) * 128],
                                                      start=True, stop=True), reads=[kwR, qT], writes=[pA if o < 4 else pB])
                    pw = rot(Pw, "pw")
                    if omin < 4:
                        c.op("act", lambda e: e.activation(out=pw[:, omin * 128:512], in_=pA[:, omin * 128:512], func=AF.Exp), reads=[pA], writes=[pw])
                    c.op("act", lambda e: e.activation(out=pw[:, 512:640], in_=pB[:, 0:128], func=AF.Exp), reads=[pB], writes=[pw])
                    c.op("dve", lambda e: e.tensor_tensor(out=pw[:, omin * 128:640], in0=pw[:, omin * 128:640], in1=patW[:, h, omin * 128:640], op=ALU.mult),
                         reads=[pw, patW], writes=[pw])
                    po = next_ps()
                    for o in range(omin, 5):
                        kt = qi128 - 4 + o
                        sl, sub = (kt // 4) % 3, kt % 4
                        c.op("pe", lambda e: e.matmul(po[:, 0:65], lhsT=pw[:, o * 128:(o + 1) * 128], rhs=vwR[:, sl, sub, :],
                                                      start=(o == omin), stop=(o == 4)), reads=[pw, vwR], writes=[po])
                    finish_branch(po, 64, s, h, 16 + h, False)
            ost = Buf(c, None, "ost")
            c.dma("pool", o_d.t[Q0:Q0 + 512, :].rearrange("(s p) f -> p s f", p=128), o_acc[:].rearrange("p s h d -> p s (h d)"), o_acc,
                  reads=[o_acc], writes=[o_d])
        c.finish("sp", [o_d, o_acc])
        print("ATTN ninst", c.ninst, "nwait", c.nwait, "nsem", c.nsem)
    return nc


def _bucket(n):
    n = np.maximum(n, 0)
    nf = np.maximum(n, 1).astype(np.float32)
    large = 16 + (np.log(nf / np.float32(16)) / np.float32(math.log(8.0)) * np.float32(16)).astype(np.int32)
    large = np.minimum(large, 31)
    return np.where(n < 16, n, large)


def _biaspat(rb, dist, valid, g):
    b = _bucket(dist)
    out = rb[b][..., g * 8:(g + 1) * 8]
    return np.where(valid[..., None], out, np.float32(NEG)).astype(np.float32)


def attn_static():
    r = np.arange(128)[:, None]
    cc = np.arange(512)[None, :]
    jp = cc - 256
    cur = (r >= 64).astype(np.int64)
    f1 = jp == cur
    f2 = jp == cur - 1
    fut = jp > cur
    AW = np.where(f1 | f2 | fut, 0.0, 1.0).astype(np.float32)
    BW = np.where(f1, 3.0e9, np.where(f2, 2.0e9, np.where(fut, -1.0, 0.0))).astype(np.float32)
    p = np.arange(128)[:, None]
    col = np.arange(4096)[None, :]
    kk, key = col // 128, col % 128
    E32 = ((p % 64) == 2 * kk + (key >= 64)).astype(np.float32)

    def f(diff):
        return ((diff >= 0) & (diff < 4)).astype(np.float32) + ((diff + 1 >= 0) & (diff + 1 < 4)).astype(np.float32)
    nn = np.arange(64)[:, None]
    OVW = f(nn - 32 - 4 * (cc[:, :512] - 256)).astype(np.float32)
    n = np.arange(1024)[:, None]
    j = np.arange(256)[None, :]
    ov = f(n - 4 * j).astype(np.float32)
    ov[1023, :] = 0.0
    ident = np.eye(128, dtype=np.float32)
    rowmask = (np.arange(64) >= 32).astype(np.float32)[:, None]
    return {"AW": AW, "BW": BW, "E32": E32, "OVW": np.ascontiguousarray(OVW), "ov": ov, "ident": ident, "rowmask": rowmask}


def _xT(inp, b, cache):
    if cache is None:
        return np.ascontiguousarray(inp["x"][b].T)
    if b not in cache:
        cache[b] = np.ascontiguousarray(inp["x"][b].T)
    return cache[b]


def attn_inputs(inp, core, static, xT_cache=None):
    b, g = core // 4, core % 4
    W = inp["nsa_w_in"][0]
    QW, KW, VW = 32 * 96, 4 * 96, 4 * 64
    offs = np.cumsum([0, QW, KW, VW, KW, VW, KW, VW])
    o_q, o_kc, o_vc, o_ks, o_vs, o_kw, o_vw, o_g = offs
    gate_cols = np.concatenate([o_g + br * 32 + g * 8 + np.arange(8) for br in range(3)])
    cols = np.concatenate([o_q + g * 768 + np.arange(768), o_kc + g * 96 + np.arange(96), o_vc + g * 64 + np.arange(64),
                           o_ks + g * 96 + np.arange(96), o_kw + g * 96 + np.arange(96),
                           o_vs + g * 64 + np.arange(64), o_vw + g * 64 + np.arange(64), gate_cols])
    rb = inp["rel_bias"]
    nn = np.arange(64)[:, None]
    tq = np.arange(512)[None, :]
    dC = tq - 16 * nn + 481
    patC = _biaspat(rb, dC, dC >= 0, g).transpose(0, 2, 1)
    kkk = np.arange(128)[:, None, None]
    oo = np.arange(5)[None, :, None]
    t128 = np.arange(128)[None, None, :]
    dW = t128 - kkk + (4 - oo) * 128
    patW = _biaspat(rb, dW, (dW >= 0) & (dW < 512), g)
    patW = patW.transpose(0, 3, 1, 2).reshape(128, 8, 640)
    o5 = np.arange(5)[:, None, None]
    k128 = np.arange(128)[None, :, None]
    t512 = np.arange(512)[None, None, :]
    dS = t512 - (o5 - 1) * 128 - k128
    patS = _biaspat(rb, dS, dS >= 0, g).transpose(0, 3, 1, 2)
    bias31 = np.broadcast_to(rb[31, g * 8:(g + 1) * 8][None, :], (128, 8))
    d = {
        "xT": _xT(inp, b, xT_cache),
        "Wg": np.ascontiguousarray(W[:, cols]),
        "gmix": np.ascontiguousarray(inp["norm_mix"][0].reshape(16, 128).T),
        "ck_w1": inp["nsa_ck_w1"][0], "cv_w1": inp["nsa_cv_w1"][0], "ck_w2": inp["nsa_ck_w2"][0], "cv_w2": inp["nsa_cv_w2"][0],
        "posT_k": np.ascontiguousarray(inp["nsa_pos_k"][0].T), "posT_v": np.ascontiguousarray(inp["nsa_pos_v"][0].T),
        "b1k": np.ascontiguousarray(inp["nsa_ck_b1"][0].reshape(2, 128).T), "b1v": np.ascontiguousarray(inp["nsa_cv_b1"][0].reshape(2, 128).T),
        "patC": np.ascontiguousarray(patC), "patW": np.ascontiguousarray(patW), "patS": np.ascontiguousarray(patS),
        "bias31": np.ascontiguousarray(bias31),
    }
    d.update(static)
    return {k: (v if (v.dtype == np.float32 and v.flags['C_CONTIGUOUS']) else np.ascontiguousarray(v, dtype=np.float32)) for k, v in d.items()}


D = 2048
DFF = 5632
NTOK = 4096
HALO = 16
EPS = 1e-6
POOLW = (2, 4, 8, 16)


def build_l2(ntok=NTOK, tile_n=512):
    nc = bass.Bass("TRN2", target_bir_lowering=False)
    TT = ntok + HALO
    with ExitStack() as st:
        c = Ctx(nc, st)
        xT = c.dram("xT", [D, TT], F32, "ExternalInput")
        oT = c.dram("oT", [D, TT], F32, "ExternalInput")
        w_out0 = c.dram("w_out0", [D, D], F32, "ExternalInput")
        w_gu = [c.dram(f"w_gu{i}", [D, 2 * DFF], F32, "ExternalInput") for i in range(2)]
        w_dn = [c.dram(f"w_dn{i}", [DFF, D], F32, "ExternalInput") for i in range(2)]
        p_in = c.dram("p_in", [D, D], F32, "ExternalInput")
        p_grp = c.dram("p_grp", [4, 512, 512], F32, "ExternalInput")
        p_out = c.dram("p_out", [D, D], F32, "ExternalInput")
        gains_d = c.dram("gains", [128, 5 * 16], F32, "ExternalInput")
        invc_d = c.dram("invc", [128, 4, ntok], F32, "ExternalInput")
        hmask_d = c.dram("hmask", [128, 1], F32, "ExternalInput")
        yT = c.dram("yT", [D, ntok], F32, "ExternalOutput")

        xres_t = st.enter_context(nc.sbuf_tensor("xres", [128, 16, tile_n], F32))
        A_t = st.enter_context(nc.sbuf_tensor("A", [128, 16, tile_n], BF16))
        B_t = st.enter_context(nc.sbuf_tensor("B", [128, 16, tile_n], BF16))
        a_t = st.enter_context(nc.sbuf_tensor("a", [128, 22, tile_n], BF16))
        u_t = st.enter_context(nc.sbuf_tensor("u", [128, 16, HALO + tile_n], F32))
        xres = [Buf(c, xres_t[:, k, :], f"xres{k}") for k in range(16)]
        A = [Buf(c, A_t[:, k, :], f"A{k}") for k in range(16)]
        B = [Buf(c, B_t[:, k, :], f"B{k}") for k in range(16)]
        a = [Buf(c, a_t[:, k, :], f"a{k}") for k in range(22)]
        u = [Buf(c, u_t[:, k, :], f"u{k}") for k in range(16)]
        wb_t = [st.enter_context(nc.sbuf_tensor(f"wb{i}", [128, 8192], BF16)) for i in range(3)]
        wb = [(wb_t[i], Buf(c, wb_t[i][:, 0:4096], f"wb{i}a"), Buf(c, wb_t[i][:, 4096:8192], f"wb{i}b")) for i in range(3)]
        ps = [c.psum(f"ps{i}", [128, 512]) for i in range(8)]
        ones = c.sbuf("ones", [128, 128], BF16)
        gains = c.sbuf("gains_sb", [128, 5 * 16], F32)
        epsb = c.sbuf("epsb", [128, 1], F32)
        hmask = c.sbuf("hmask_sb", [128, 1], F32)
        sq = [c.sbuf(f"sq{i}", [128, tile_n], BF16) for i in range(2)]
        rstd = c.sbuf("rstd", [128, tile_n], F32)
        sg = [c.sbuf(f"sg{i}", [128, tile_n], F32) for i in range(3)]
        invc = c.sbuf("invc_sb", [128, 4, tile_n], F32)
        tA = c.sbuf("tA", [128, HALO + tile_n], F32)
        tB = c.sbuf("tB", [128, HALO + tile_n], F32)
        xld = Buf(c, None, "xld")
        old = Buf(c, None, "old")
        yst = Buf(c, None, "yst")

        state = {"ps": 0, "wb": 0, "sg": 0}

        def next_ps():
            p = ps[state["ps"] % 8]
            state["ps"] += 1
            return p

        def next_wb():
            w = wb[state["wb"] % 3]
            state["wb"] += 1
            return w

        def next_sg():
            s = sg[state["sg"] % 3]
            state["sg"] += 1
            return s

        c.op("pool", lambda e: e.memset(ones[:], 1.0), writes=[ones])
        c.op("pool", lambda e: e.memset(epsb[:], EPS), writes=[epsb])
        c.dma("sp", gains[:], gains_d[:], gains, reads=[gains_d], writes=[gains])
        c.dma("sp", hmask[:], hmask_d[:], hmask, reads=[hmask_d], writes=[hmask])

        def gcol(gi, k):
            return gains[:, gi * 16 + k: gi * 16 + k + 1]

        def rmsnorm(n, gi, outs, out_col0=0):
            pst = next_ps()
            for k in range(16):
                s = sq[k % 2]
                c.op("act", lambda e: e.activation(out=s[:, :n], in_=xres[k][:, :n], func=AF.Square),
                     reads=[xres[k]], writes=[s])
                c.op("pe", lambda e: e.matmul(pst[:, :n], lhsT=ones[:], rhs=s[:, :n], start=(k == 0), stop=(k == 15)),
                     reads=[ones, s], writes=[pst])
            c.op("act", lambda e: e.activation(out=rstd[:, :n], in_=pst[:, :n], func=AF.Sqrt, bias=epsb[:, 0:1], scale=1.0 / D),
                 reads=[pst, epsb], writes=[rstd])
            c.op("dve", lambda e: e.reciprocal(out=rstd[:, :n], in_=rstd[:, :n]), reads=[rstd], writes=[rstd])
            for k in range(16):
                c.op("dve", lambda e: e.scalar_tensor_tensor(out=outs[k][:, out_col0:out_col0 + n], in0=xres[k][:, :n],
                                                            scalar=gcol(gi, k), in1=rstd[:, :n],
                                                            op0=ALU.mult, op1=ALU.mult),
                     reads=[xres[k], gains, rstd], writes=[outs[k]])

        def linear(W, KC, row0, ncols_total, cb, ins, n, evac):
            nblk = ncols_total // cb
            for mb in range(nblk):
                wt, wa, wb2 = next_wb()
                w = wa
                wv = wt[:, 0:KC * cb].rearrange("p (k m) -> p k m", m=cb)
                src = W[row0:row0 + KC * 128, mb * cb:(mb + 1) * cb].rearrange("(k p) m -> p k m", p=128)
                c.dma("pool", wv, src, wa, reads=[], writes=[wa, wb2])
                for mi in range(cb // 128):
                    p = next_ps()
                    for k in range(KC):
                        c.op("pe", lambda e: e.matmul(p[:, :n], lhsT=wv[:, k, mi * 128:(mi + 1) * 128], rhs=ins[k][:, :n],
                                                      start=(k == 0), stop=(k == KC - 1)),
                             reads=[wa, wb2, ins[k]], writes=[p])
                    evac(mb * (cb // 128) + mi, p)

        def add_to_xres(n):
            def ev(m, p):
                c.op("dve", lambda e: e.tensor_tensor(out=xres[m][:, :n], in0=p[:, :n], in1=xres[m][:, :n], op=ALU.add),
                     reads=[p, xres[m]], writes=[xres[m]])
            return ev

        def ffn(li, ins, n):
            Wgu = w_gu[li].t
            for half in range(2):
                for blk in range(11):
                    f0 = (half * 22 + blk * 2) * 128
                    wt, wa, wb2 = next_wb()
                    wvg = wt[:, 0:4096].rearrange("p (k m) -> p k m", m=256)
                    wvu = wt[:, 4096:8192].rearrange("p (k m) -> p k m", m=256)
                    c.dma("pool", wvg, Wgu[:, f0:f0 + 256].rearrange("(k p) m -> p k m", p=128), wa, reads=[], writes=[wa])
                    c.dma("pool", wvu, Wgu[:, DFF + f0:DFF + f0 + 256].rearrange("(k p) m -> p k m", p=128), wb2, reads=[], writes=[wb2])
                    for j in range(2):
                        pg = next_ps()
                        for k in range(16):
                            c.op("pe", lambda e: e.matmul(pg[:, :n], lhsT=wvg[:, k, j * 128:(j + 1) * 128], rhs=ins[k][:, :n],
                                                          start=(k == 0), stop=(k == 15)), reads=[wa, ins[k]], writes=[pg])
                        pu = next_ps()
                        for k in range(16):
                            c.op("pe", lambda e: e.matmul(pu[:, :n], lhsT=wvu[:, k, j * 128:(j + 1) * 128], rhs=ins[k][:, :n],
                                                          start=(k == 0), stop=(k == 15)), reads=[wb2, ins[k]], writes=[pu])
                        s = next_sg()
                        c.op("act", lambda e: e.activation(out=s[:, :n], in_=pg[:, :n], func=AF.Silu), reads=[pg], writes=[s])
                        ac = a[blk * 2 + j]
                        c.op("dve", lambda e: e.tensor_tensor(out=ac[:, :n], in0=pu[:, :n], in1=s[:, :n], op=ALU.mult),
                             reads=[pu, s], writes=[ac])
                linear(w_dn[li].t, 22, half * 2816, D, 256, a, n, add_to_xres(n))

        tiles = [(0, HALO)] + [(HALO + i * tile_n, tile_n) for i in range(ntok // tile_n)]
        for ti, (c0, n) in enumerate(tiles):
            c.dma("sp", xres_t[:, :, :n], xT.t[:, c0:c0 + n].rearrange("(k p) t -> p k t", p=128), xld,
                  reads=[xT], writes=xres + [xld])
            c.dma("pool", A_t[:, :, :n], oT.t[:, c0:c0 + n].rearrange("(k p) t -> p k t", p=128), old,
                  reads=[oT], writes=A + [old])
            linear(w_out0.t, 16, 0, D, 512, A, n, add_to_xres(n))
            rmsnorm(n, 0, B)
            ffn(0, B, n)
            rmsnorm(n, 1, A)

            def ev_u(m, p):
                c.op("act", lambda e: e.activation(out=u[m][:, HALO:HALO + n], in_=p[:, :n], func=AF.Copy),
                     reads=[p], writes=[u[m]])
            linear(p_in.t, 16, 0, D, 512, A, n, ev_u)
            if ti == 0:
                for m in range(16):
                    c.op("dve", lambda e: e.tensor_scalar(out=u[m][:, 0:HALO], in0=u[m][:, HALO:2 * HALO], scalar1=hmask[:, 0:1],
                                                         scalar2=None, op0=ALU.mult), reads=[u[m], hmask], writes=[u[m]])
                continue
            t0 = c0 - HALO
            c.dma("sp", invc[:, :, :n], invc_d.t[:, :, t0:t0 + n], invc, reads=[invc_d], writes=[invc])
            L = HALO + n
            for m in range(16):
                gi = m // 4
                w_ = POOLW[gi]
                U = u[m]
                cur, lo, k = U, 0, 1
                tmps = [tA, tB]
                ti_ = 0
                while k < w_:
                    dst = tmps[ti_ % 2]
                    ti_ += 1
                    lo2 = lo + k
                    c.op("dve", lambda e: e.tensor_tensor(out=dst[:, lo2:L], in0=cur[:, lo2:L], in1=cur[:, lo2 - k:L - k], op=ALU.add),
                         reads=[cur], writes=[dst])
                    cur, lo, k = dst, lo2, k * 2
                other = tmps[ti_ % 2]
                c.op("dve", lambda e: e.tensor_tensor(out=other[:, HALO:L], in0=cur[:, HALO:L], in1=invc[:, gi, :n], op=ALU.mult),
                     reads=[cur, invc], writes=[other])
                c.op("dve", lambda e: e.tensor_tensor(out=B[m][:, :n], in0=other[:, HALO:L], in1=U[:, HALO:L], op=ALU.subtract),
                     reads=[other, U], writes=[B[m]])
                c.op("dve", lambda e: e.tensor_copy(out=U[:, 0:HALO], in_=U[:, n:n + HALO]), reads=[U], writes=[U])
            for gi in range(4):
                def ev_g(m, p, gi=gi):
                    mm = gi * 4 + m
                    c.op("act", lambda e: e.activation(out=A[mm][:, :n], in_=p[:, :n], func=AF.Copy, scale=gcol(4, mm)),
                         reads=[p, gains], writes=[A[mm]])
                linear(p_grp.t[gi], 4, 0, 512, 512, B[gi * 4:gi * 4 + 4], n, ev_g)
            linear(p_out.t, 16, 0, D, 512, A, n, add_to_xres(n))
            rmsnorm(n, 2, B)
            ffn(1, B, n)
            rmsnorm(n, 3, u, out_col0=HALO)
            c.dma("sp", yT.t[:, t0:t0 + n].rearrange("(k p) t -> p k t", p=128), u_t[:, :, HALO:HALO + n], yst,
                  reads=u, writes=[yT, yst])
        c.finish("sp", [yT, yst])
        print("L2 ninst", c.ninst, "nwait", c.nwait, "nsem", c.nsem)
    return nc


def l2_inputs(x, o, inp, core):
    b, j = core // 4, core % 4
    t0 = j * NTOK
    S = x.shape[1]

    def slab(arr):
        out = np.zeros((D, NTOK + HALO), np.float32)
        lo = t0 - HALO
        if lo >= 0:
            out[:, :] = arr[b, lo:t0 + NTOK, :].T
        else:
            out[:, HALO:] = arr[b, t0:t0 + NTOK, :].T
        return out
    gl = lambda v: np.ascontiguousarray(v.reshape(16, 128).T)
    gains = np.concatenate([gl(inp["norm_ffn"][0]), gl(inp["norm_mix"][1]), gl(inp["norm_ffn"][1]),
                            gl(inp["norm_final"]), gl(inp["pool_scale"][0])], axis=1).astype(np.float32)
    tglob = t0 + np.arange(NTOK)
    invc = np.stack([1.0 / np.minimum(tglob + 1, w).astype(np.float32) for w in POOLW], axis=0).astype(np.float32)
    invc = np.ascontiguousarray(np.broadcast_to(invc[None], (128, 4, NTOK)))
    hmask = np.full((128, 1), 0.0 if j == 0 else 1.0, np.float32)
    return {
        "xT": slab(x), "oT": slab(o),
        "w_out0": inp["nsa_w_out"][0], "w_gu0": inp["ffn_w_gu"][0], "w_gu1": inp["ffn_w_gu"][1],
        "w_dn0": inp["ffn_w_down"][0], "w_dn1": inp["ffn_w_down"][1],
        "p_in": inp["pool_w_in"][0], "p_grp": inp["pool_w_grp"][0], "p_out": inp["pool_w_out"][0],
        "gains": gains, "invc": invc, "hmask": hmask,
    }


def kernel(**inputs):
    inp = {k: np.asarray(v) for k, v in inputs.items()}
    static = attn_static()
    nc1 = build_attn()
    maps1 = []
    xT_cache = {}
    for core in range(8):
        m = attn_inputs(inp, core, static, xT_cache)
        maps1.append(m)
    r1 = run_bass_kernel_spmd(nc1, maps1, core_ids=list(range(8)))
    o = np.empty((2, S, D), np.float32)
    for core in range(8):
        b, g = core // 4, core % 4
        o[b, :, g * 512:(g + 1) * 512] = r1.results[core]["o"]
    del maps1, xT_cache
    nc2 = build_l2()
    maps2 = [l2_inputs(inp["x"], o, inp, core) for core in range(8)]
    r2 = run_bass_kernel_spmd(nc2, maps2, core_ids=list(range(8)))
    out = np.empty((2, S, D), np.float32)
    for core in range(8):
        b, j = core // 4, core % 4
        out[b, j * NTOK:(j + 1) * NTOK, :] = r2.results[core]["yT"].T
    return out
```

```python
import math
from contextlib import ExitStack
import numpy as np
import concourse.bass as bass
import concourse.mybir as mybir
from concourse.bass_utils import run_bass_kernel_spmd


F32 = mybir.dt.float32
BF16 = mybir.dt.bfloat16
ALU = mybir.AluOpType
AF = mybir.ActivationFunctionType
AX = mybir.AxisListType

EPOCH = 30000


class Buf:
    def __init__(self, ctx, t, name):
        self.ctx = ctx
        self.t = t
        self.name = name
        self.last_write = None
        self.reads = []
        self.dma_sem = None
        self.dma_cnt = 0
        self.is_psum = False

    def __getitem__(self, idx):
        return self.t[idx]


class Ctx:
    def __init__(self, nc, stack):
        self.nc = nc
        self.stack = stack
        self.sem_stack = stack
        self.eng = {"pe": nc.tensor, "dve": nc.vector, "act": nc.scalar,
                    "pool": nc.gpsimd, "sp": nc.sync}
        self.cur_sem = {}
        self.cnt = {}
        self.waited = {e: {} for e in self.eng}
        self.nsem = 0
        self.ninst = {e: 0 for e in self.eng}
        self.nwait = 0
        self.owners = []
        self.dead = False
        for e in self.eng:
            if e == "sp":
                continue
            self.cur_sem[e] = self._newsem("s_" + e)
            self.cnt[e] = 0

    def _newsem(self, name):
        self.nsem += 1
        return self.sem_stack.enter_context(self.nc.semaphore(f"{name}_{self.nsem}"))

    def sbuf(self, name, shape, dtype):
        t = self.stack.enter_context(self.nc.sbuf_tensor(name, list(shape), dtype))
        return Buf(self, t, name)

    def psum(self, name, shape, dtype=F32):
        t = self.stack.enter_context(self.nc.psum_tensor(name, list(shape), dtype))
        b = Buf(self, t, name)
        b.is_psum = True
        return b

    def dram(self, name, shape, dtype, kind):
        t = self.nc.dram_tensor(name, list(shape), dtype, kind=kind)
        return Buf(self, t.ap(), name)

    def _wait(self, e, deps):
        eng = self.eng[e]
        w = self.waited[e]
        best = {}
        for d in deps:
            if d is None:
                continue
            sem, val, src = d
            if src == "pe" and e == "pe":
                continue
            k = id(sem)
            if w.get(k, 0) >= val:
                continue
            if k not in best or best[k][1] < val:
                best[k] = (sem, val)
        for k, (sem, val) in best.items():
            eng.wait_ge(sem, val)
            w[k] = val
            self.nwait += 1

    def _deps(self, reads, writes):
        deps = []
        for b in reads:
            deps.append(b.last_write)
        for b in writes:
            deps.append(b.last_write)
            deps.extend(b.reads)
        return deps

    def _commit(self, ev, reads, writes):
        for b in reads:
            b.reads.append(ev)
            if len(b.reads) > 64:
                best = {}
                for s, v, src in b.reads:
                    if id(s) not in best or best[id(s)][1] < v:
                        best[id(s)] = (s, v, src)
                b.reads = list(best.values())
        for b in writes:
            b.last_write = ev
            b.reads = []

    def op(self, e, fn, reads=(), writes=()):
        if self.dead:
            return None
        deps = self._deps(reads, writes)
        for b in reads:
            if b.is_psum:
                deps.extend(r for r in b.reads if r[2] != e)
        self._wait(e, deps)
        ins = fn(self.eng[e])
        if self.cnt[e] >= EPOCH:
            self.cur_sem[e] = self._newsem("s_" + e)
            self.cnt[e] = 0
        self.cnt[e] += 1
        sem = self.cur_sem[e]
        ins.then_inc(sem, 1)
        ev = (sem, self.cnt[e], e)
        self._commit(ev, reads, writes)
        self.ninst[e] += 1
        return ev

    def dma(self, q, out_ap, in_ap, owner, reads=(), writes=(), **kw):
        if self.dead:
            return None
        self._wait(q, self._deps(reads, writes))
        if owner.dma_sem is None:
            owner.dma_sem = self._newsem("d_" + owner.name)
            self.owners.append(owner)
        ins = self.eng[q].dma_start(out=out_ap, in_=in_ap, **kw)
        owner.dma_cnt += 16
        ins.then_inc(owner.dma_sem, 16)
        ev = (owner.dma_sem, owner.dma_cnt, "dma")
        self._commit(ev, reads, writes)
        self.ninst[q] += 1
        return ev

    def finish(self, e, bufs):
        if self.dead:
            return
        deps = []
        for b in bufs:
            deps.append(b.last_write)
            deps.extend(b.reads)
        self._wait(e, deps)

    def barrier(self):
        if self.dead:
            return
        deps = [(self.cur_sem[e], self.cnt[e], e) for e in self.cur_sem if self.cnt[e] > 0]
        deps += [(b.dma_sem, b.dma_cnt, "dma") for b in self.owners]
        for q in self.eng:
            self._wait(q, [d for d in deps if not (d[2] == q)])


S = 16384
D = 2048
NCOL = 1272
FM_COLS = 1120
TM_COLS = 152
EPS = 1e-6
NEG = -30000.0


class _Stop(Exception):
    pass


def build_attn(nqt=S // 512, stop=None):
    nc = bass.Bass("TRN2", target_bir_lowering=False)
    with ExitStack() as st:
      try:
        _build_attn(nc, st, nqt, stop)
      except _Stop:
        pass
    return nc


def _build_attn(nc, st, nqt, stop):
    if True:
        c = Ctx(nc, st)
        IN = "ExternalInput"
        xT = c.dram("xT", [D, S], F32, IN)
        Wg = c.dram("Wg", [D, NCOL], F32, IN)
        gmix = c.dram("gmix", [128, 16], F32, IN)
        ck_w1 = c.dram("ck_w1", [3072, 256], F32, IN)
        cv_w1 = c.dram("cv_w1", [2048, 256], F32, IN)
        ck_w2 = c.dram("ck_w2", [256, 96], F32, IN)
        cv_w2 = c.dram("cv_w2", [256, 64], F32, IN)
        posT_k = c.dram("posT_k", [96, 32], F32, IN)
        posT_v = c.dram("posT_v", [64, 32], F32, IN)
        b1k_d = c.dram("b1k", [128, 2], F32, IN)
        b1v_d = c.dram("b1v", [128, 2], F32, IN)
        patC_d = c.dram("patC", [64, 8, 512], F32, IN)
        patW_d = c.dram("patW", [128, 8, 640], F32, IN)
        patS_d = c.dram("patS", [5, 8, 128, 512], F32, IN)
        bias31_d = c.dram("bias31", [128, 8], F32, IN)
        AW_d = c.dram("AW", [128, 512], F32, IN)
        BW_d = c.dram("BW", [128, 512], F32, IN)
        E32_d = c.dram("E32", [128, 4096], F32, IN)
        OVW_d = c.dram("OVW", [64, 512], F32, IN)
        ov_d = c.dram("ov", [1024, 256], F32, IN)
        ident_d = c.dram("ident", [128, 128], F32, IN)
        rowmask_d = c.dram("rowmask", [64, 1], F32, IN)
        o_d = c.dram("o", [S, 512], F32, "ExternalOutput")
        q_d = c.dram("q_scr", [8, 96, S], BF16, "ExternalOutput")
        ksT_d = c.dram("ks_scr", [96, S], BF16, "ExternalOutput")
        kwT_d = c.dram("kw_scr", [96, S], BF16, "ExternalOutput")
        vsw_d = c.dram("vsw_scr", [S, 128], BF16, "ExternalOutput")
        gates_d = c.dram("gates_scr", [S, 24], F32, "ExternalOutput")
        scratch_events = []

        ps = [c.psum(f"ps{i}", [128, 512]) for i in range(8)]
        pstate = {"i": 0, "lo": 0, "n": 8}

        def next_ps():
            p = ps[pstate["lo"] + pstate["i"] % pstate["n"]]
            pstate["i"] += 1
            return p

        kcT = c.sbuf("kcT", [96, 32 + 1024], BF16)
        vcat = c.sbuf("vcat", [128, 8, 321], BF16)
        vcB = c.sbuf("vcB", [64, 32, 65], BF16)
        ones = c.sbuf("ones", [128, 128], BF16)
        one32 = c.sbuf("one32", [1, 1], F32)
        epsb = c.sbuf("epsb", [128, 1], F32)
        ident = c.sbuf("ident_sb", [128, 128], F32)
        c.op("pool", lambda e: e.memset(ones[:], 1.0), writes=[ones])
        c.op("pool", lambda e: e.memset(one32[:], 1.0), writes=[one32])
        c.op("pool", lambda e: e.memset(epsb[:], EPS), writes=[epsb])
        c.op("pool", lambda e: e.memset(kcT[:], 0.0), writes=[kcT])
        c.op("pool", lambda e: e.memset(vcat[:], 0.0), writes=[vcat])
        c.op("pool", lambda e: e.memset(vcB[:], 0.0), writes=[vcB])
        c.dma("sp", ident[:], ident_d[:], ident, reads=[], writes=[ident])

        with ExitStack() as sB:
            kTc_t = sB.enter_context(nc.sbuf_tensor("kTc", [96, 16, 1024], BF16))
            vTc_t = sB.enter_context(nc.sbuf_tensor("vTc", [64, 16, 1024], BF16))
            kTc = Buf(c, kTc_t, "kTc")
            vTc = Buf(c, vTc_t, "vTc")
            c.op("pool", lambda e: e.memset(kTc_t[:], 0.0), writes=[kTc])
            c.op("pool", lambda e: e.memset(vTc_t[:], 0.0), writes=[vTc])
            with ExitStack() as sA:
                c.stack = sA
                Wsb = c.sbuf("Wsb", [128, 16, NCOL], BF16)
                gm = c.sbuf("gm", [128, 16], F32)
                c.dma("sp", gm[:], gmix[:], gm, writes=[gm])
                wst = [c.sbuf(f"wst{i}", [128, 16, 128], F32) for i in range(2)]
                nblk = (NCOL + 127) // 128
                for bi in range(nblk):
                    c0 = bi * 128
                    wd = min(128, NCOL - c0)
                    w = wst[bi % 2]
                    c.dma("sp", w[:, :, :wd], Wg.t[:, c0:c0 + wd].rearrange("(k p) m -> p k m", p=128), w, writes=[w])
                    for k in range(16):
                        c.op("dve" if k % 2 == 0 else "pool",
                             lambda e: e.tensor_scalar(out=Wsb[:, k, c0:c0 + wd], in0=w[:, k, :wd], scalar1=gm[:, k:k + 1],
                                                       scalar2=None, op0=ALU.mult), reads=[w, gm], writes=[Wsb])
                if stop == 'p1w':
                    c.barrier(); c.dead = True
                xb = [c.sbuf(f"xb{i}", [128, 16, 512], BF16) for i in range(2)]
                sq = [c.sbuf(f"sq{i}", [128, 512], BF16) for i in range(2)]
                rstd = c.sbuf("rstd", [128, 512], F32)
                rcol = c.sbuf("rcol", [128, 4], F32)
                qst = [c.sbuf(f"qst{i}", [96, 8, 512], BF16) for i in range(2)]
                kst = [c.sbuf(f"kst{i}", [96, 512], BF16) for i in range(2)]
                kwt = [c.sbuf(f"kwt{i}", [96, 512], BF16) for i in range(2)]
                vst = [c.sbuf(f"vst{i}", [128, 4, 128], BF16) for i in range(2)]
                gst = [c.sbuf(f"gst{i}", [128, 4, 24], F32) for i in range(2)]
                for ti in range(nqt):
                    T0 = ti * 512
                    x = xb[ti % 2]
                    c.dma("pool", x[:], xT.t[:, T0:T0 + 512].rearrange("(k p) t -> p k t", p=128), x, writes=[x])
                    pss = next_ps()
                    for k in range(16):
                        s_ = sq[k % 2]
                        c.op("act", lambda e: e.activation(out=s_[:], in_=x[:, k, :], func=AF.Square), reads=[x], writes=[s_])
                        c.op("pe", lambda e: e.matmul(pss[:], lhsT=ones[:], rhs=s_[:], start=(k == 0), stop=(k == 15)),
                             reads=[ones, s_], writes=[pss])
                    c.op("act", lambda e: e.activation(out=rstd[:], in_=pss[:], func=AF.Sqrt, bias=epsb[:, 0:1], scale=1.0 / D),
                         reads=[pss, epsb], writes=[rstd])
                    c.op("dve", lambda e: e.reciprocal(out=rstd[:], in_=rstd[:]), reads=[rstd], writes=[rstd])
                    pc = next_ps()
                    for s in range(4):
                        c.op("pe", lambda e: e.matmul(pc[:, s:s + 1], lhsT=rstd[0:1, s * 128:(s + 1) * 128], rhs=one32[0:1, 0:1],
                                                      start=True, stop=True), reads=[rstd, one32], writes=[pc])
                    c.op("act", lambda e: e.activation(out=rcol[:], in_=pc[:, 0:4], func=AF.Copy), reads=[pc], writes=[rcol])
                    if stop == 'p1s':
                        c.barrier(); c.dead = True
                    qs, ks_, kw_ = qst[ti % 2], kst[ti % 2], kwt[ti % 2]
                    groups = [(h * 96, 96, ("q", h)) for h in range(8)] + [(768, 96, ("kc", 0)), (864, 64, ("vc", 0)),
                                                                          (928, 96, ("ks", 0)), (1024, 96, ("kw", 0))]
                    for (c0, wd, (kind, h)) in groups:
                        p = next_ps()
                        for k in range(16):
                            c.op("pe", lambda e: e.matmul(p[0:wd, :], lhsT=Wsb[:, k, c0:c0 + wd], rhs=x[:, k, :],
                                                          start=(k == 0), stop=(k == 15)), reads=[Wsb, x], writes=[p])
                        if kind == "q":
                            c.op("dve", lambda e: e.scalar_tensor_tensor(out=qs[0:96, h, :], in0=p[0:96, :], scalar=96.0 ** -0.5,
                                                                        in1=rstd[0:96, :], op0=ALU.mult, op1=ALU.mult),
                                 reads=[p, rstd], writes=[qs])
                        elif kind in ("kc", "vc"):
                            dst_t, dstB = (kTc_t, kTc) if kind == "kc" else (vTc_t, vTc)
                            ch0 = ti * 32
                            c.op("dve", lambda e: e.tensor_tensor(
                                out=dst_t[0:wd, :, ch0:ch0 + 32].rearrange("p r c -> p c r"),
                                in0=p[0:wd, :].rearrange("p (c r) -> p c r", r=16),
                                in1=rstd[0:wd, :].rearrange("p (c r) -> p c r", r=16), op=ALU.mult),
                                reads=[p, rstd], writes=[dstB])
                        else:
                            dst = ks_ if kind == "ks" else kw_
                            c.op("dve", lambda e: e.tensor_tensor(out=dst[0:96, :], in0=p[0:96, :], in1=rstd[0:96, :], op=ALU.mult),
                                 reads=[p, rstd], writes=[dst])
                    if stop == 'p1f':
                        c.barrier(); c.dead = True
                    vs_, gs_ = vst[ti % 2], gst[ti % 2]
                    for s in range(4):
                        p = next_ps()
                        for k in range(16):
                            c.op("pe", lambda e: e.matmul(p[:, 0:TM_COLS], lhsT=x[:, k, s * 128:(s + 1) * 128], rhs=Wsb[:, k, FM_COLS:NCOL],
                                                          start=(k == 0), stop=(k == 15)), reads=[Wsb, x], writes=[p])
                        c.op("dve", lambda e: e.tensor_scalar(out=vs_[:, s, :], in0=p[:, 0:128], scalar1=rcol[:, s:s + 1], scalar2=None,
                                                             op0=ALU.mult), reads=[p, rcol], writes=[vs_])
                        c.op("act", lambda e: e.activation(out=gs_[:, s, :], in_=p[:, 128:152], func=AF.Sigmoid, scale=rcol[:, s:s + 1]),
                             reads=[p, rcol, vs_], writes=[gs_])
                    if stop == 'p1t':
                        c.barrier(); c.dead = True
                    scratch_events.append(c.dma("sp", q_d.t[:, :, T0:T0 + 512].rearrange("h d t -> d h t"), qs[:], qs, reads=[qs]))
                    scratch_events.append(c.dma("sp", ksT_d.t[:, T0:T0 + 512], ks_[:], ks_, reads=[ks_]))
                    scratch_events.append(c.dma("sp", kwT_d.t[:, T0:T0 + 512], kw_[:], kw_, reads=[kw_]))
                    scratch_events.append(c.dma("sp", vsw_d.t[T0:T0 + 512, :].rearrange("(s p) c -> p s c", p=128), vs_[:], vs_, reads=[vs_]))
                    scratch_events.append(c.dma("sp", gates_d.t[T0:T0 + 512, :].rearrange("(s p) c -> p s c", p=128), gs_[:], gs_, reads=[gs_]))
                c.barrier()
                if stop == 'p1':
                    c.dead = True
                c.stack = st
            with ExitStack() as sC:
                c.stack = sC
                for (nm, dk, w1_d, w2_d, posT_d, b1_d, src_t, srcB) in (("k", 96, ck_w1, ck_w2, posT_k, b1k_d, kTc_t, kTc),
                                                                        ("v", 64, cv_w1, cv_w2, posT_v, b1v_d, vTc_t, vTc)):
                    with ExitStack() as sD:
                        c.stack = sD
                        W1 = c.sbuf("W1" + nm, [dk, 32, 256], BF16)
                        c.dma("pool", W1[:], w1_d.t.rearrange("(l d) j -> d l j", d=dk), W1, writes=[W1])
                        W2 = c.sbuf("W2" + nm, [128, 2, dk], BF16)
                        c.dma("pool", W2[:], w2_d.t.rearrange("(jh p) d -> p jh d", p=128), W2, writes=[W2])
                        pos = c.sbuf("pos" + nm, [dk, 32], BF16)
                        c.dma("pool", pos[:], posT_d[:], pos, writes=[pos])
                        b1 = c.sbuf("b1s" + nm, [128, 2], F32)
                        c.dma("sp", b1[:], b1_d[:], b1, writes=[b1])
                        cst = c.sbuf("cst" + nm, [128, 2], F32)
                        pcst = next_ps()
                        for jh in range(2):
                            for l in range(32):
                                c.op("pe", lambda e: e.matmul(pcst[:, jh:jh + 1], lhsT=W1[:, l, jh * 128:(jh + 1) * 128], rhs=pos[:, l:l + 1],
                                                              start=(l == 0), stop=(l == 31)), reads=[W1, pos], writes=[pcst])
                        c.op("dve", lambda e: e.tensor_tensor(out=cst[:], in0=pcst[:, 0:2], in1=b1[:], op=ALU.add), reads=[pcst, b1], writes=[cst])
                        hid = c.sbuf("hid" + nm, [128, 2, 32 + 1024], BF16)
                        c.op("pool", lambda e: e.memset(hid[:], 0.0), writes=[hid])
                        xh = c.sbuf("xh" + nm, [128, 512], F32)
                        tt_ = c.sbuf("tt" + nm, [128, 512], F32)
                        for jh in range(2):
                            for nb in range(2):
                                N = 512 if nb == 0 else 511
                                p = next_ps()
                                for l in range(32):
                                    a0 = nb * 512 + (l // 16)
                                    c.op("pe", lambda e: e.matmul(p[:, 0:N], lhsT=W1[:, l, jh * 128:(jh + 1) * 128], rhs=src_t[0:dk, l % 16, a0:a0 + N],
                                                                  start=(l == 0), stop=(l == 31)), reads=[W1, srcB], writes=[p])
                                c.op("act", lambda e: e.activation(out=xh[:, 0:N], in_=p[:, 0:N], func=AF.Identity, bias=cst[:, jh:jh + 1]),
                                     reads=[p, cst], writes=[xh])
                                c.op("dve", lambda e: e.tensor_tensor(out=tt_[:, 0:N], in0=xh[:, 0:N], in1=xh[:, 0:N], op=ALU.mult), reads=[xh], writes=[tt_])
                                c.op("dve", lambda e: e.tensor_scalar(out=tt_[:, 0:N], in0=tt_[:, 0:N], scalar1=0.044715, scalar2=1.0, op0=ALU.mult, op1=ALU.add),
                                     reads=[tt_], writes=[tt_])
                                c.op("dve", lambda e: e.tensor_tensor(out=tt_[:, 0:N], in0=tt_[:, 0:N], in1=xh[:, 0:N], op=ALU.mult), reads=[tt_, xh], writes=[tt_])
                                c.op("act", lambda e: e.activation(out=tt_[:, 0:N], in_=tt_[:, 0:N], func=AF.Sigmoid, scale=2.0 * math.sqrt(2.0 / math.pi)),
                                     reads=[tt_], writes=[tt_])
                                c.op("dve", lambda e: e.tensor_tensor(out=hid[:, jh, 32 + nb * 512:32 + nb * 512 + N], in0=tt_[:, 0:N], in1=xh[:, 0:N], op=ALU.mult),
                                     reads=[tt_, xh], writes=[hid])
                        if nm == "k":
                            for nb in range(2):
                                p = next_ps()
                                for jh in range(2):
                                    c.op("pe", lambda e: e.matmul(p[0:96, :], lhsT=W2[:, jh, :], rhs=hid[:, jh, 32 + nb * 512:32 + (nb + 1) * 512],
                                                                  start=(jh == 0), stop=(jh == 1)), reads=[W2, hid], writes=[p])
                                c.op("act", lambda e: e.activation(out=kcT[0:96, 32 + nb * 512:32 + (nb + 1) * 512], in_=p[0:96, :], func=AF.Copy),
                                     reads=[p], writes=[kcT])
                        else:
                            for i in range(8):
                                p = next_ps()
                                for jh in range(2):
                                    c.op("pe", lambda e: e.matmul(p[:, 0:64], lhsT=hid[:, jh, 32 + i * 128:32 + (i + 1) * 128], rhs=W2[:, jh, :],
                                                                  start=(jh == 0), stop=(jh == 1)), reads=[W2, hid], writes=[p])
                                c.op("act", lambda e: e.activation(out=vcat[:, i, 0:64], in_=p[:, 0:64], func=AF.Copy), reads=[p], writes=[vcat])
                            for qi in range(32):
                                p = next_ps()
                                for jh in range(2):
                                    c.op("pe", lambda e: e.matmul(p[0:64, 0:64], lhsT=hid[:, jh, 32 * qi:32 * qi + 64], rhs=W2[:, jh, :],
                                                                  start=(jh == 0), stop=(jh == 1)), reads=[W2, hid], writes=[p])
                                c.op("act", lambda e: e.activation(out=vcB[0:64, qi, 0:64], in_=p[0:64, 0:64], func=AF.Copy), reads=[p], writes=[vcB])
                        c.barrier()
                        c.stack = sC
                c.stack = st
        if stop == 'p1b':
            c.barrier()
            c.dead = True
        c.op("pool", lambda e: e.memset(vcat[:, :, 64:65], 1.0), writes=[vcat])
        c.op("pool", lambda e: e.memset(vcB[:, :, 64:65], 1.0), writes=[vcB])
        ovl = Buf(c, None, "ovl")
        c.dma("pool", vcat[:, :, 65:321], ov_d.t.rearrange("(i p) j -> p i j", p=128), ovl, writes=[vcat, ovl])

        if not c.dead:
            for q in ("sp", "pool", "act"):
                c._wait(q, scratch_events)
        ksT = c.sbuf("ksT", [96, S], BF16)
        vs = c.sbuf("vs", [128, S // 128, 65], BF16)
        c.op("pool", lambda e: e.memset(vs[:, :, 64:65], 1.0), writes=[vs])
        Sx = nqt * 512
        nch = (Sx + 4095) // 4096
        for i in range(nch):
            c1 = min(Sx, (i + 1) * 4096)
            c.dma("sp", ksT[:, i * 4096:c1], ksT_d.t[:, i * 4096:c1], Buf(c, None, f"ksl{i}"), writes=[ksT])
        nch = (Sx + 2047) // 2048
        for i in range(nch):
            r1 = min(Sx, (i + 1) * 2048)
            c.dma("sp", vs[:, i * 16:r1 // 128, 0:64], vsw_d.t[i * 2048:r1, 0:64].rearrange("(k p) d -> p k d", p=128),
                  Buf(c, None, f"vsl{i}"), writes=[vs])
        stage = c.sbuf("stage", [128, 640], F32)
        patC = c.sbuf("patC_sb", [64, 8, 512], BF16)
        patW = c.sbuf("patW_sb", [128, 8, 640], BF16)
        for h in range(8):
            c.dma("sp", stage[0:64, 0:512], patC_d.t[:, h, :], stage, writes=[stage])
            c.op("act", lambda e: e.activation(out=patC[:, h, :], in_=stage[0:64, 0:512], func=AF.Exp), reads=[stage], writes=[patC])
            c.dma("sp", stage[:, :], patW_d.t[:, h, :], stage, writes=[stage])
            c.op("act", lambda e: e.activation(out=patW[:, h, :], in_=stage[:, :], func=AF.Exp), reads=[stage], writes=[patW])
        bias31 = c.sbuf("bias31_sb", [128, 8], F32)
        c.dma("sp", bias31[:], bias31_d[:], bias31, writes=[bias31])
        AW = c.sbuf("AW_sb", [128, 512], F32)
        BW = c.sbuf("BW_sb", [128, 512], F32)
        c.dma("sp", AW[:], AW_d[:], AW, writes=[AW])
        c.dma("sp", BW[:], BW_d[:], BW, writes=[BW])
        E32 = c.sbuf("E32_sb", [128, 4096], BF16)
        c.dma("pool", E32[:], E32_d[:], E32, writes=[E32])
        OVW = c.sbuf("OVW_sb", [64, 512], BF16)
        c.dma("pool", OVW[:], OVW_d[:], OVW, writes=[OVW])
        rowmask = c.sbuf("rowmask_sb", [64, 1], F32)
        c.dma("sp", rowmask[:], rowmask_d[:], rowmask, writes=[rowmask])

        if stop == 'p2setup':
            c.barrier()
            c.dead = True
        qT = c.sbuf("qT", [96, 8, 512], BF16)
        kwR = c.sbuf("kwR", [96, 3, 512], BF16)
        vwR = c.sbuf("vwR", [128, 3, 4, 65], BF16)
        c.op("pool", lambda e: e.memset(vwR[:, :, :, 64:65], 1.0), writes=[vwR])
        gates = c.sbuf("gates_sb", [128, 4, 24], F32)
        Pc = [c.sbuf(f"Pc{i}", [128, 512], BF16) for i in range(8)]
        Pb = c.sbuf("Pb", [64, 512], BF16)
        selT = [c.sbuf(f"selT{i}", [128, 512], BF16) for i in range(2)]
        mT = [c.sbuf(f"mT{i}", [128, 512], BF16) for i in range(2)]
        Pt = [c.sbuf(f"Pt{i}", [128, 512], BF16) for i in range(4)]
        pbias = [c.sbuf(f"pbias{i}", [128, 512], F32) for i in range(3)]
        scb = [c.sbuf(f"scb{i}", [128, 512], F32) for i in range(2)]
        Pw = [c.sbuf(f"Pw{i}", [128, 640], BF16) for i in range(3)]
        o_acc = c.sbuf("o_acc", [128, 4, 8, 64], F32)
        imp = c.sbuf("imp", [128, 4, 256], F32)
        osT = [c.sbuf(f"osT{i}", [65, 512], F32) for i in range(2)]
        impm = c.sbuf("impm", [128, 256], F32)
        wk2 = c.sbuf("wk2", [128, 256], F32)
        sel = c.sbuf("sel", [128, 256], F32)
        m1 = c.sbuf("m1", [128, 8], F32)
        m2 = c.sbuf("m2", [128, 8], F32)
        rden = [c.sbuf(f"rden{i}", [128, 1], F32) for i in range(4)]
        scl = [c.sbuf(f"scl{i}", [128, 1], F32) for i in range(4)]
        cnt = {"r": 0, "pt": 0, "pb": 0, "scb": 0, "mt": 0, "pw": 0, "os": 0}

        def rot(lst, key):
            v = lst[cnt[key] % len(lst)]
            cnt[key] += 1
            return v

        def finish_branch(pso, dcol, s, h, gidx, first):
            rd, sc = rot(rden, "r"), scl[(cnt["r"] - 1) % 4]
            c.op("dve", lambda e: e.tensor_scalar(out=rd[:], in0=pso[:, dcol:dcol + 1], scalar1=1e-30, scalar2=None, op0=ALU.add),
                 reads=[pso], writes=[rd])
            c.op("dve", lambda e: e.reciprocal(out=rd[:], in_=rd[:]), reads=[rd], writes=[rd])
            c.op("dve", lambda e: e.tensor_tensor(out=sc[:], in0=rd[:], in1=gates[:, s, gidx:gidx + 1], op=ALU.mult), reads=[rd, gates], writes=[sc])
            if first:
                c.op("act", lambda e: e.activation(out=o_acc[:, s, h, :], in_=pso[:, 0:64], func=AF.Copy, scale=sc[:, 0:1]),
                     reads=[pso, sc], writes=[o_acc])
            else:
                c.op("dve", lambda e: e.scalar_tensor_tensor(out=o_acc[:, s, h, :], in0=pso[:, 0:64], scalar=sc[:, 0:1], in1=o_acc[:, s, h, :],
                                                            op0=ALU.mult, op1=ALU.add), reads=[pso, sc, o_acc], writes=[o_acc])
            return rd

        for Qi in range(nqt):
            Q0 = Qi * 512
            slot = Qi % 3
            c.dma("sp", qT[:], q_d.t[:, :, Q0:Q0 + 512].rearrange("h d t -> d h t"), qT, writes=[qT])
            kwl = Buf(c, None, "kwl")
            c.dma("sp", kwR[:, slot, :], kwT_d.t[:, Q0:Q0 + 512], kwR, writes=[kwR])
            vwl = Buf(c, None, "vwl")
            c.dma("sp", vwR[:, slot, :, 0:64], vsw_d.t[Q0:Q0 + 512, 64:128].rearrange("(s p) d -> p s d", p=128), vwR, writes=[vwR])
            c.dma("sp", gates[:], gates_d.t[Q0:Q0 + 512, :].rearrange("(s p) g -> p s g", p=128), gates, writes=[gates])

            NF = max(0, 32 * Qi - 32)
            nfar = (NF + 127) // 128
            for h in range(8):
                for i in range(nfar):
                    rows = min(128, NF - 128 * i)
                    p = next_ps()
                    c.op("pe", lambda e: e.matmul(p[0:rows, :], lhsT=kcT[0:96, 32 + 128 * i:32 + 128 * i + rows], rhs=qT[0:96, h, :],
                                                  start=True, stop=True), reads=[kcT, qT], writes=[p])
                    c.op("act", lambda e: e.activation(out=Pc[i][0:rows, :], in_=p[0:rows, :], func=AF.Exp, bias=bias31[0:rows, h:h + 1]),
                         reads=[p, bias31], writes=[Pc[i]])
                p = next_ps()
                c.op("pe", lambda e: e.matmul(p[0:64, :], lhsT=kcT[0:96, 32 * Qi:32 * Qi + 64], rhs=qT[0:96, h, :], start=True, stop=True),
                     reads=[kcT, qT], writes=[p])
                c.op("act", lambda e: e.activation(out=Pb[:], in_=p[0:64, :], func=AF.Exp), reads=[p], writes=[Pb])
                c.op("dve", lambda e: e.tensor_tensor(out=Pb[:], in0=Pb[:], in1=patC[:, h, :], op=ALU.mult), reads=[Pb, patC], writes=[Pb])
                if Qi == 0:
                    c.op("dve", lambda e: e.tensor_scalar(out=Pb[:], in0=Pb[:], scalar1=rowmask[:, 0:1], scalar2=None, op0=ALU.mult),
                         reads=[Pb, rowmask], writes=[Pb])
                for s in range(4):
                    po, po2 = next_ps(), next_ps()
                    for i in range(nfar):
                        rows = min(128, NF - 128 * i)
                        c.op("pe", lambda e: e.matmul(po[:, 0:65], lhsT=Pc[i][0:rows, s * 128:(s + 1) * 128], rhs=vcat[0:rows, i, 0:65],
                                                      start=(i == 0), stop=False), reads=[Pc[i], vcat], writes=[po])
                    c.op("pe", lambda e: e.matmul(po[:, 0:65], lhsT=Pb[0:64, s * 128:(s + 1) * 128], rhs=vcB[0:64, Qi, :],
                                                  start=(nfar == 0), stop=True), reads=[Pb, vcB], writes=[po])
                    for i in range(nfar):
                        rows = min(128, NF - 128 * i)
                        c.op("pe", lambda e: e.matmul(po2[:, 0:256], lhsT=Pc[i][0:rows, s * 128:(s + 1) * 128], rhs=vcat[0:rows, i, 65:321],
                                                      start=(i == 0), stop=False), reads=[Pc[i], vcat], writes=[po2])
                    c.op("pe", lambda e: e.matmul(po2[:, 0:256], lhsT=Pb[0:64, s * 128:(s + 1) * 128], rhs=OVW[0:64, 256 - 8 * Qi:512 - 8 * Qi],
                                                  start=(nfar == 0), stop=True), reads=[Pb, OVW], writes=[po2])
                    rd = finish_branch(po, 64, s, h, 0 * 8 + h, True)
                    if h == 0:
                        c.op("dve", lambda e: e.tensor_scalar(out=imp[:, s, :], in0=po2[:, 0:256], scalar1=rd[:, 0:1], scalar2=None, op0=ALU.mult),
                             reads=[po2, rd], writes=[imp])
                    else:
                        c.op("dve", lambda e: e.scalar_tensor_tensor(out=imp[:, s, :], in0=po2[:, 0:256], scalar=rd[:, 0:1], in1=imp[:, s, :],
                                                                    op0=ALU.mult, op1=ALU.add), reads=[po2, rd, imp], writes=[imp])
            if stop == 'p2c':
                c.barrier()
                c.dead = True
            for s in range(4):
                qi128 = 4 * Qi + s
                lo = 256 - 2 * qi128
                c.op("dve", lambda e: e.tensor_tensor(out=impm[:], in0=imp[:, s, :], in1=AW[:, lo:lo + 256], op=ALU.mult), reads=[imp, AW], writes=[impm])
                c.op("dve", lambda e: e.tensor_tensor(out=impm[:], in0=impm[:], in1=BW[:, lo:lo + 256], op=ALU.add), reads=[impm, BW], writes=[impm])
                c.op("dve", lambda e: e.memset(impm[:, 0:1], 4.0e9), writes=[impm])
                c.op("dve", lambda e: e.max(out=m1[:], in_=impm[:]), reads=[impm], writes=[m1])
                c.op("dve", lambda e: e.match_replace(out=wk2[:], in_to_replace=m1[:], in_values=impm[:], imm_value=-2.0), reads=[m1, impm], writes=[wk2])
                c.op("dve", lambda e: e.max(out=m2[:], in_=wk2[:]), reads=[wk2], writes=[m2])
                c.op("dve", lambda e: e.tensor_scalar(out=sel[:], in0=impm[:], scalar1=m2[:, 7:8], scalar2=None, op0=ALU.is_ge), reads=[impm, m2], writes=[sel])
                for jh in range(2):
                    p = next_ps()
                    c.op("pe", lambda e: e.transpose(out=p[:, 0:128], in_=sel[:, jh * 128:(jh + 1) * 128], identity=ident[:]), reads=[sel, ident], writes=[p])
                    c.op("act", lambda e: e.activation(out=selT[jh][:, s * 128:(s + 1) * 128], in_=p[:, 0:128], func=AF.Copy), reads=[p], writes=[selT[jh]])
            if stop == 'p2sel':
                c.barrier()
                c.dead = True
            nkt = 4 * Qi + 4
            pstate["lo"], pstate["n"], pstate["i"] = 4, 4, 0
            SK = 2
            for hg in range(2):
                units = [(kt, hh) for kt in range(nkt) for hh in range(4)]
                pend = []
                mcur = {}

                def stageA(kt, hh):
                    if hh == 0:
                        jh, a_, kk = kt // 64, (kt // 32) % 2, kt % 32
                        pm = next_ps()
                        c.op("pe", lambda e: e.matmul(pm[:], lhsT=E32[64 * a_:64 * a_ + 64, kk * 128:(kk + 1) * 128], rhs=selT[jh][64 * a_:64 * a_ + 64, :],
                                                      start=True, stop=True), reads=[E32, selT[jh]], writes=[pm])
                        m_ = rot(mT, "mt")
                        c.op("dve", lambda e: e.tensor_copy(out=m_[:], in_=pm[:]), reads=[pm], writes=[m_])
                        mcur["m"] = m_
                    m = mcur["m"]
                    near = kt >= 4 * Qi - 1
                    o = kt - (4 * Qi - 1)
                    h = hg * 4 + hh
                    p = next_ps()
                    c.op("pe", lambda e: e.matmul(p[:], lhsT=ksT[0:96, kt * 128:(kt + 1) * 128], rhs=qT[0:96, h, :], start=True, stop=True),
                         reads=[ksT, qT], writes=[p])
                    P = rot(Pt, "pt")
                    if not near:
                        c.op("act", lambda e: e.activation(out=P[:], in_=p[:], func=AF.Exp, bias=bias31[:, h:h + 1]), reads=[p, bias31], writes=[P])
                    else:
                        pb = rot(pbias, "pb")
                        c.dma("sp", pb[:], patS_d.t[o, h], pb, writes=[pb])
                        sb = rot(scb, "scb")
                        c.op("dve", lambda e: e.tensor_tensor(out=sb[:], in0=p[:], in1=pb[:], op=ALU.add), reads=[p, pb], writes=[sb])
                        c.op("act", lambda e: e.activation(out=P[:], in_=sb[:], func=AF.Exp), reads=[sb], writes=[P])
                    c.op("dve", lambda e: e.tensor_tensor(out=P[:], in0=P[:], in1=m[:], op=ALU.mult), reads=[P, m], writes=[P])
                    return P

                def stageB(kt, hh, P):
                    c.op("pe", lambda e: e.matmul(ps[hh][0:65, :], lhsT=vs[:, kt, :], rhs=P[:], start=(kt == 0), stop=(kt == nkt - 1)),
                         reads=[vs, P], writes=[ps[hh]])

                for i in range(len(units) + SK):
                    if i < len(units):
                        kt, hh = units[i]
                        pend.append((kt, hh, stageA(kt, hh)))
                    if i >= SK:
                        stageB(*pend[i - SK])
                for hh in range(4):
                    h = hg * 4 + hh
                    os_ = rot(osT, "os")
                    c.op("act", lambda e: e.activation(out=os_[:], in_=ps[hh][0:65, :], func=AF.Copy), reads=[ps[hh]], writes=[os_])
                    for s in range(4):
                        p = next_ps()
                        c.op("pe", lambda e: e.transpose(out=p[:, 0:65], in_=os_[0:65, s * 128:(s + 1) * 128], identity=ident[0:65, 0:65]),
                             reads=[os_, ident], writes=[p])
                        finish_branch(p, 64, s, h, 8 + h, False)
            pstate["lo"], pstate["n"], pstate["i"] = 0, 8, 0
            if stop == 'p2s':
                c.barrier()
                c.dead = True
            wunits = [(s, h) for s in range(4) for h in range(8)]
            wpend = []

            def wA(s, h):
                qi128 = 4 * Qi + s
                omin = max(0, 4 - qi128)
                pA, pB = next_ps(), next_ps()
                for o in range(omin, 5):
                    kt = qi128 - 4 + o
                    sl, sub = (kt // 4) % 3, kt % 4
                    dst = pA[:, o * 128:(o + 1) * 128] if o < 4 else pB[:, 0:128]
                    c.op("pe", lambda e: e.matmul(dst, lhsT=kwR[0:96, sl, sub * 128:(sub + 1) * 128], rhs=qT[0:96, h, s * 128:(s + 1) * 128],
                                                  start=True, stop=True), reads=[kwR, qT], writes=[pA if o < 4 else pB])
                pw = rot(Pw, "pw")
                if omin < 4:
                    c.op("act", lambda e: e.activation(out=pw[:, omin * 128:512], in_=pA[:, omin * 128:512], func=AF.Exp), reads=[pA], writes=[pw])
                c.op("act", lambda e: e.activation(out=pw[:, 512:640], in_=pB[:, 0:128], func=AF.Exp), reads=[pB], writes=[pw])
                c.op("dve", lambda e: e.tensor_tensor(out=pw[:, omin * 128:640], in0=pw[:, omin * 128:640], in1=patW[:, h, omin * 128:640], op=ALU.mult),
                     reads=[pw, patW], writes=[pw])
                return pw

            def wB(s, h, pw):
                qi128 = 4 * Qi + s
                omin = max(0, 4 - qi128)
                po = next_ps()
                for o in range(omin, 5):
                    kt = qi128 - 4 + o
                    sl, sub = (kt // 4) % 3, kt % 4
                    c.op("pe", lambda e: e.matmul(po[:, 0:65], lhsT=pw[:, o * 128:(o + 1) * 128], rhs=vwR[:, sl, sub, :],
                                                  start=(o == omin), stop=(o == 4)), reads=[pw, vwR], writes=[po])
                finish_branch(po, 64, s, h, 16 + h, False)

            for i in range(len(wunits) + 1):
                if i < len(wunits):
                    s_, h_ = wunits[i]
                    wpend.append((s_, h_, wA(s_, h_)))
                if i >= 1:
                    wB(*wpend[i - 1])
            ost = Buf(c, None, "ost")
            c.dma("pool", o_d.t[Q0:Q0 + 512, :].rearrange("(s p) f -> p s f", p=128), o_acc[:].rearrange("p s h d -> p s (h d)"), o_acc,
                  reads=[o_acc], writes=[o_d])
        c.finish("sp", [o_d, o_acc])
    return nc


def _bucket(n):
    n = np.maximum(n, 0)
    nf = np.maximum(n, 1).astype(np.float32)
    large = 16 + (np.log(nf / np.float32(16)) / np.float32(math.log(8.0)) * np.float32(16)).astype(np.int32)
    large = np.minimum(large, 31)
    return np.where(n < 16, n, large)


def _biaspat(rb, dist, valid, g):
    b = _bucket(dist)
    out = rb[b][..., g * 8:(g + 1) * 8]
    return np.where(valid[..., None], out, np.float32(NEG)).astype(np.float32)


def attn_static():
    r = np.arange(128)[:, None]
    cc = np.arange(512)[None, :]
    jp = cc - 256
    cur = (r >= 64).astype(np.int64)
    f1 = jp == cur
    f2 = jp == cur - 1
    fut = jp > cur
    AW = np.where(f1 | f2 | fut, 0.0, 1.0).astype(np.float32)
    BW = np.where(f1, 3.0e9, np.where(f2, 2.0e9, np.where(fut, -1.0, 0.0))).astype(np.float32)
    p = np.arange(128)[:, None]
    col = np.arange(4096)[None, :]
    kk, key = col // 128, col % 128
    E32 = ((p % 64) == 2 * kk + (key >= 64)).astype(np.float32)

    def f(diff):
        return ((diff >= 0) & (diff < 4)).astype(np.float32) + ((diff + 1 >= 0) & (diff + 1 < 4)).astype(np.float32)
    nn = np.arange(64)[:, None]
    OVW = f(nn - 32 - 4 * (cc[:, :512] - 256)).astype(np.float32)
    n = np.arange(1024)[:, None]
    j = np.arange(256)[None, :]
    ov = f(n - 4 * j).astype(np.float32)
    ov[1023, :] = 0.0
    ident = np.eye(128, dtype=np.float32)
    rowmask = (np.arange(64) >= 32).astype(np.float32)[:, None]
    return {"AW": AW, "BW": BW, "E32": E32, "OVW": np.ascontiguousarray(OVW), "ov": ov, "ident": ident, "rowmask": rowmask}


def _xT(inp, b, cache):
    if cache is None:
        return np.ascontiguousarray(inp["x"][b].T)
    if b not in cache:
        cache[b] = np.ascontiguousarray(inp["x"][b].T)
    return cache[b]


def attn_inputs(inp, core, static, xT_cache=None):
    b, g = core // 4, core % 4
    W = inp["nsa_w_in"][0]
    QW, KW, VW = 32 * 96, 4 * 96, 4 * 64
    offs = np.cumsum([0, QW, KW, VW, KW, VW, KW, VW])
    o_q, o_kc, o_vc, o_ks, o_vs, o_kw, o_vw, o_g = offs
    gate_cols = np.concatenate([o_g + br * 32 + g * 8 + np.arange(8) for br in range(3)])
    cols = np.concatenate([o_q + g * 768 + np.arange(768), o_kc + g * 96 + np.arange(96), o_vc + g * 64 + np.arange(64),
                           o_ks + g * 96 + np.arange(96), o_kw + g * 96 + np.arange(96),
                           o_vs + g * 64 + np.arange(64), o_vw + g * 64 + np.arange(64), gate_cols])
    rb = inp["rel_bias"]
    nn = np.arange(64)[:, None]
    tq = np.arange(512)[None, :]
    dC = tq - 16 * nn + 481
    patC = _biaspat(rb, dC, dC >= 0, g).transpose(0, 2, 1)
    kkk = np.arange(128)[:, None, None]
    oo = np.arange(5)[None, :, None]
    t128 = np.arange(128)[None, None, :]
    dW = t128 - kkk + (4 - oo) * 128
    patW = _biaspat(rb, dW, (dW >= 0) & (dW < 512), g)
    patW = patW.transpose(0, 3, 1, 2).reshape(128, 8, 640)
    o5 = np.arange(5)[:, None, None]
    k128 = np.arange(128)[None, :, None]
    t512 = np.arange(512)[None, None, :]
    dS = t512 - (o5 - 1) * 128 - k128
    patS = _biaspat(rb, dS, dS >= 0, g).transpose(0, 3, 1, 2)
    bias31 = np.broadcast_to(rb[31, g * 8:(g + 1) * 8][None, :], (128, 8))
    d = {
        "xT": _xT(inp, b, xT_cache),
        "Wg": np.ascontiguousarray(W[:, cols]),
        "gmix": np.ascontiguousarray(inp["norm_mix"][0].reshape(16, 128).T),
        "ck_w1": inp["nsa_ck_w1"][0], "cv_w1": inp["nsa_cv_w1"][0], "ck_w2": inp["nsa_ck_w2"][0], "cv_w2": inp["nsa_cv_w2"][0],
        "posT_k": np.ascontiguousarray(inp["nsa_pos_k"][0].T), "posT_v": np.ascontiguousarray(inp["nsa_pos_v"][0].T),
        "b1k": np.ascontiguousarray(inp["nsa_ck_b1"][0].reshape(2, 128).T), "b1v": np.ascontiguousarray(inp["nsa_cv_b1"][0].reshape(2, 128).T),
        "patC": np.ascontiguousarray(patC), "patW": np.ascontiguousarray(patW), "patS": np.ascontiguousarray(patS),
        "bias31": np.ascontiguousarray(bias31),
    }
    d.update(static)
    return {k: (v if (v.dtype == np.float32 and v.flags['C_CONTIGUOUS']) else np.ascontiguousarray(v, dtype=np.float32)) for k, v in d.items()}


D = 2048
DFF = 5632
NTOK = 4096
HALO = 16
EPS = 1e-6
POOLW = (2, 4, 8, 16)


def build_l2(ntok=NTOK, tile_n=512):
    nc = bass.Bass("TRN2", target_bir_lowering=False)
    TT = ntok + HALO
    with ExitStack() as st:
        c = Ctx(nc, st)
        xT = c.dram("xT", [D, TT], F32, "ExternalInput")
        oT = c.dram("oT", [D, TT], F32, "ExternalInput")
        w_out0 = c.dram("w_out0", [D, D], F32, "ExternalInput")
        w_gu = [c.dram(f"w_gu{i}", [D, 2 * DFF], F32, "ExternalInput") for i in range(2)]
        w_dn = [c.dram(f"w_dn{i}", [DFF, D], F32, "ExternalInput") for i in range(2)]
        p_in = c.dram("p_in", [D, D], F32, "ExternalInput")
        p_grp = c.dram("p_grp", [4, 512, 512], F32, "ExternalInput")
        p_out = c.dram("p_out", [D, D], F32, "ExternalInput")
        gains_d = c.dram("gains", [128, 5 * 16], F32, "ExternalInput")
        invc_d = c.dram("invc", [128, 4, ntok], F32, "ExternalInput")
        hmask_d = c.dram("hmask", [128, 1], F32, "ExternalInput")
        yT = c.dram("yT", [D, ntok], F32, "ExternalOutput")

        xres_t = st.enter_context(nc.sbuf_tensor("xres", [128, 16, tile_n], F32))
        A_t = st.enter_context(nc.sbuf_tensor("A", [128, 16, tile_n], BF16))
        B_t = st.enter_context(nc.sbuf_tensor("B", [128, 16, tile_n], BF16))
        a_t = st.enter_context(nc.sbuf_tensor("a", [128, 22, tile_n], BF16))
        u_t = st.enter_context(nc.sbuf_tensor("u", [128, 16, HALO + tile_n], F32))
        xres = [Buf(c, xres_t[:, k, :], f"xres{k}") for k in range(16)]
        A = [Buf(c, A_t[:, k, :], f"A{k}") for k in range(16)]
        B = [Buf(c, B_t[:, k, :], f"B{k}") for k in range(16)]
        a = [Buf(c, a_t[:, k, :], f"a{k}") for k in range(22)]
        u = [Buf(c, u_t[:, k, :], f"u{k}") for k in range(16)]
        wb_t = [st.enter_context(nc.sbuf_tensor(f"wb{i}", [128, 8192], BF16)) for i in range(3)]
        wb = [(wb_t[i], Buf(c, wb_t[i][:, 0:4096], f"wb{i}a"), Buf(c, wb_t[i][:, 4096:8192], f"wb{i}b")) for i in range(3)]
        ps = [c.psum(f"ps{i}", [128, 512]) for i in range(8)]
        ones = c.sbuf("ones", [128, 128], BF16)
        gains = c.sbuf("gains_sb", [128, 5 * 16], F32)
        epsb = c.sbuf("epsb", [128, 1], F32)
        hmask = c.sbuf("hmask_sb", [128, 1], F32)
        sq = [c.sbuf(f"sq{i}", [128, tile_n], BF16) for i in range(2)]
        rstd = c.sbuf("rstd", [128, tile_n], F32)
        sg = [c.sbuf(f"sg{i}", [128, tile_n], F32) for i in range(3)]
        invc = c.sbuf("invc_sb", [128, 4, tile_n], F32)
        tA = c.sbuf("tA", [128, HALO + tile_n], F32)
        tB = c.sbuf("tB", [128, HALO + tile_n], F32)
        xld = Buf(c, None, "xld")
        old = Buf(c, None, "old")
        yst = Buf(c, None, "yst")

        state = {"ps": 0, "wb": 0, "sg": 0}

        def next_ps():
            p = ps[state["ps"] % 8]
            state["ps"] += 1
            return p

        def next_wb():
            w = wb[state["wb"] % 3]
            state["wb"] += 1
            return w

        def next_sg():
            s = sg[state["sg"] % 3]
            state["sg"] += 1
            return s

        c.op("pool", lambda e: e.memset(ones[:], 1.0), writes=[ones])
        c.op("pool", lambda e: e.memset(epsb[:], EPS), writes=[epsb])
        c.dma("sp", gains[:], gains_d[:], gains, reads=[gains_d], writes=[gains])
        c.dma("sp", hmask[:], hmask_d[:], hmask, reads=[hmask_d], writes=[hmask])

        def gcol(gi, k):
            return gains[:, gi * 16 + k: gi * 16 + k + 1]

        def rmsnorm(n, gi, outs, out_col0=0):
            pst = next_ps()
            for k in range(16):
                s = sq[k % 2]
                c.op("act", lambda e: e.activation(out=s[:, :n], in_=xres[k][:, :n], func=AF.Square),
                     reads=[xres[k]], writes=[s])
                c.op("pe", lambda e: e.matmul(pst[:, :n], lhsT=ones[:], rhs=s[:, :n], start=(k == 0), stop=(k == 15)),
                     reads=[ones, s], writes=[pst])
            c.op("act", lambda e: e.activation(out=rstd[:, :n], in_=pst[:, :n], func=AF.Sqrt, bias=epsb[:, 0:1], scale=1.0 / D),
                 reads=[pst, epsb], writes=[rstd])
            c.op("dve", lambda e: e.reciprocal(out=rstd[:, :n], in_=rstd[:, :n]), reads=[rstd], writes=[rstd])
            for k in range(16):
                c.op("dve", lambda e: e.scalar_tensor_tensor(out=outs[k][:, out_col0:out_col0 + n], in0=xres[k][:, :n],
                                                            scalar=gcol(gi, k), in1=rstd[:, :n],
                                                            op0=ALU.mult, op1=ALU.mult),
                     reads=[xres[k], gains, rstd], writes=[outs[k]])

        def linear(W, KC, row0, ncols_total, cb, ins, n, evac):
            nblk = ncols_total // cb
            for mb in range(nblk):
                wt, wa, wb2 = next_wb()
                w = wa
                wv = wt[:, 0:KC * cb].rearrange("p (k m) -> p k m", m=cb)
                src = W[row0:row0 + KC * 128, mb * cb:(mb + 1) * cb].rearrange("(k p) m -> p k m", p=128)
                c.dma("pool", wv, src, wa, reads=[], writes=[wa, wb2])
                for mi in range(cb // 128):
                    p = next_ps()
                    for k in range(KC):
                        c.op("pe", lambda e: e.matmul(p[:, :n], lhsT=wv[:, k, mi * 128:(mi + 1) * 128], rhs=ins[k][:, :n],
                                                      start=(k == 0), stop=(k == KC - 1)),
                             reads=[wa, wb2, ins[k]], writes=[p])
                    evac(mb * (cb // 128) + mi, p)

        def add_to_xres(n):
            def ev(m, p):
                c.op("dve", lambda e: e.tensor_tensor(out=xres[m][:, :n], in0=p[:, :n], in1=xres[m][:, :n], op=ALU.add),
                     reads=[p, xres[m]], writes=[xres[m]])
            return ev

        def ffn(li, ins, n):
            Wgu = w_gu[li].t
            for half in range(2):
                for blk in range(11):
                    f0 = (half * 22 + blk * 2) * 128
                    wt, wa, wb2 = next_wb()
                    wvg = wt[:, 0:4096].rearrange("p (k m) -> p k m", m=256)
                    wvu = wt[:, 4096:8192].rearrange("p (k m) -> p k m", m=256)
                    c.dma("pool", wvg, Wgu[:, f0:f0 + 256].rearrange("(k p) m -> p k m", p=128), wa, reads=[], writes=[wa])
                    c.dma("pool", wvu, Wgu[:, DFF + f0:DFF + f0 + 256].rearrange("(k p) m -> p k m", p=128), wb2, reads=[], writes=[wb2])
                    for j in range(2):
                        pg = next_ps()
                        for k in range(16):
                            c.op("pe", lambda e: e.matmul(pg[:, :n], lhsT=wvg[:, k, j * 128:(j + 1) * 128], rhs=ins[k][:, :n],
                                                          start=(k == 0), stop=(k == 15)), reads=[wa, ins[k]], writes=[pg])
                        pu = next_ps()
                        for k in range(16):
                            c.op("pe", lambda e: e.matmul(pu[:, :n], lhsT=wvu[:, k, j * 128:(j + 1) * 128], rhs=ins[k][:, :n],
                                                          start=(k == 0), stop=(k == 15)), reads=[wb2, ins[k]], writes=[pu])
                        s = next_sg()
                        c.op("act", lambda e: e.activation(out=s[:, :n], in_=pg[:, :n], func=AF.Silu), reads=[pg], writes=[s])
                        ac = a[blk * 2 + j]
                        c.op("dve", lambda e: e.tensor_tensor(out=ac[:, :n], in0=pu[:, :n], in1=s[:, :n], op=ALU.mult),
                             reads=[pu, s], writes=[ac])
                linear(w_dn[li].t, 22, half * 2816, D, 256, a, n, add_to_xres(n))

        tiles = [(0, HALO)] + [(HALO + i * tile_n, tile_n) for i in range(ntok // tile_n)]
        for ti, (c0, n) in enumerate(tiles):
            c.dma("sp", xres_t[:, :, :n], xT.t[:, c0:c0 + n].rearrange("(k p) t -> p k t", p=128), xld,
                  reads=[xT], writes=xres + [xld])
            c.dma("pool", A_t[:, :, :n], oT.t[:, c0:c0 + n].rearrange("(k p) t -> p k t", p=128), old,
                  reads=[oT], writes=A + [old])
            linear(w_out0.t, 16, 0, D, 512, A, n, add_to_xres(n))
            rmsnorm(n, 0, B)
            ffn(0, B, n)
            rmsnorm(n, 1, A)

            def ev_u(m, p):
                c.op("act", lambda e: e.activation(out=u[m][:, HALO:HALO + n], in_=p[:, :n], func=AF.Copy),
                     reads=[p], writes=[u[m]])
            linear(p_in.t, 16, 0, D, 512, A, n, ev_u)
            if ti == 0:
                for m in range(16):
                    c.op("dve", lambda e: e.tensor_scalar(out=u[m][:, 0:HALO], in0=u[m][:, HALO:2 * HALO], scalar1=hmask[:, 0:1],
                                                         scalar2=None, op0=ALU.mult), reads=[u[m], hmask], writes=[u[m]])
                continue
            t0 = c0 - HALO
            c.dma("sp", invc[:, :, :n], invc_d.t[:, :, t0:t0 + n], invc, reads=[invc_d], writes=[invc])
            L = HALO + n
            for m in range(16):
                gi = m // 4
                w_ = POOLW[gi]
                U = u[m]
                cur, lo, k = U, 0, 1
                tmps = [tA, tB]
                ti_ = 0
                while k < w_:
                    dst = tmps[ti_ % 2]
                    ti_ += 1
                    lo2 = lo + k
                    c.op("dve", lambda e: e.tensor_tensor(out=dst[:, lo2:L], in0=cur[:, lo2:L], in1=cur[:, lo2 - k:L - k], op=ALU.add),
                         reads=[cur], writes=[dst])
                    cur, lo, k = dst, lo2, k * 2
                other = tmps[ti_ % 2]
                c.op("dve", lambda e: e.tensor_tensor(out=other[:, HALO:L], in0=cur[:, HALO:L], in1=invc[:, gi, :n], op=ALU.mult),
                     reads=[cur, invc], writes=[other])
                c.op("dve", lambda e: e.tensor_tensor(out=B[m][:, :n], in0=other[:, HALO:L], in1=U[:, HALO:L], op=ALU.subtract),
                     reads=[other, U], writes=[B[m]])
                c.op("dve", lambda e: e.tensor_copy(out=U[:, 0:HALO], in_=U[:, n:n + HALO]), reads=[U], writes=[U])
            for gi in range(4):
                def ev_g(m, p, gi=gi):
                    mm = gi * 4 + m
                    c.op("act", lambda e: e.activation(out=A[mm][:, :n], in_=p[:, :n], func=AF.Copy, scale=gcol(4, mm)),
                         reads=[p, gains], writes=[A[mm]])
                linear(p_grp.t[gi], 4, 0, 512, 512, B[gi * 4:gi * 4 + 4], n, ev_g)
            linear(p_out.t, 16, 0, D, 512, A, n, add_to_xres(n))
            rmsnorm(n, 2, B)
            ffn(1, B, n)
            rmsnorm(n, 3, u, out_col0=HALO)
            c.dma("sp", yT.t[:, t0:t0 + n].rearrange("(k p) t -> p k t", p=128), u_t[:, :, HALO:HALO + n], yst,
                  reads=u, writes=[yT, yst])
        c.finish("sp", [yT, yst])
    return nc


def l2_inputs(x, o, inp, core):
    b, j = core // 4, core % 4
    t0 = j * NTOK
    S = x.shape[1]

    def slab(arr):
        out = np.zeros((D, NTOK + HALO), np.float32)
        lo = t0 - HALO
        if lo >= 0:
            out[:, :] = arr[b, lo:t0 + NTOK, :].T
        else:
            out[:, HALO:] = arr[b, t0:t0 + NTOK, :].T
        return out
    gl = lambda v: np.ascontiguousarray(v.reshape(16, 128).T)
    gains = np.concatenate([gl(inp["norm_ffn"][0]), gl(inp["norm_mix"][1]), gl(inp["norm_ffn"][1]),
                            gl(inp["norm_final"]), gl(inp["pool_scale"][0])], axis=1).astype(np.float32)
    tglob = t0 + np.arange(NTOK)
    invc = np.stack([1.0 / np.minimum(tglob + 1, w).astype(np.float32) for w in POOLW], axis=0).astype(np.float32)
    invc = np.ascontiguousarray(np.broadcast_to(invc[None], (128, 4, NTOK)))
    hmask = np.full((128, 1), 0.0 if j == 0 else 1.0, np.float32)
    return {
        "xT": slab(x), "oT": slab(o),
        "w_out0": inp["nsa_w_out"][0], "w_gu0": inp["ffn_w_gu"][0], "w_gu1": inp["ffn_w_gu"][1],
        "w_dn0": inp["ffn_w_down"][0], "w_dn1": inp["ffn_w_down"][1],
        "p_in": inp["pool_w_in"][0], "p_grp": inp["pool_w_grp"][0], "p_out": inp["pool_w_out"][0],
        "gains": gains, "invc": invc, "hmask": hmask,
    }


def kernel(**inputs):
    inp = {k: np.asarray(v) for k, v in inputs.items()}
    static = attn_static()
    nc1 = build_attn()
    maps1 = []
    xT_cache = {}
    for core in range(8):
        m = attn_inputs(inp, core, static, xT_cache)
        maps1.append(m)
    r1 = run_bass_kernel_spmd(nc1, maps1, core_ids=list(range(8)))
    o = np.empty((2, S, D), np.float32)
    for core in range(8):
        b, g = core // 4, core % 4
        o[b, :, g * 512:(g + 1) * 512] = r1.results[core]["o"]
    del maps1, xT_cache
    nc2 = build_l2()
    maps2 = [l2_inputs(inp["x"], o, inp, core) for core in range(8)]
    r2 = run_bass_kernel_spmd(nc2, maps2, core_ids=list(range(8)))
    out = np.empty((2, S, D), np.float32)
    for core in range(8):
        b, j = core // 4, core % 4
        out[b, j * NTOK:(j + 1) * NTOK, :] = r2.results[core]["yT"].T
    return out
```
